# Optimizing a Trainium2 kernel written in Bass

```python
import jax, jax.numpy as jnp
from jax import lax
import numpy as np

D_MODEL = 2048
BATCH = 4
SEQ = 4096
DEPTH = 1
DEC_BATCH = 128
DEC_SEQ = 4
PAST_LEN = 16384
PAGE_SIZE = 128

RET_HEADS = 8
RET_DK = 64
RET_DV = 128
RET_WIDTH = RET_HEADS * RET_DV
RET_CHUNK = 128
SWA_HEADS = 16
SWA_KV_HEADS = 2
SWA_DH = 64
SWA_GROUP = SWA_HEADS // SWA_KV_HEADS
SWA_WIDTH = SWA_HEADS * SWA_DH
WINDOW = 128
SWA_BLOCK = 128
MIX_WIDTH = RET_WIDTH + SWA_WIDTH
PROJ_SIZES = (RET_HEADS * RET_DK, RET_HEADS * RET_DK, RET_WIDTH, RET_WIDTH,
              SWA_WIDTH, SWA_KV_HEADS * SWA_DH, SWA_KV_HEADS * SWA_DH)
PROJ_WIDTH = 2 * RET_HEADS * RET_DK + 2 * RET_WIDTH + SWA_WIDTH + 2 * SWA_KV_HEADS * SWA_DH
D_FF = 5632
CONV_W = 3
ROPE_THETA = 10000.0
EPS = 1e-6

kernel_name = 'hymba_retention_swa_sink_convffn_step'

F32 = jnp.float32


def rmsnorm(x, w):
    xf = x.astype(F32)
    y = xf * lax.rsqrt(jnp.mean(xf * xf, axis=-1, keepdims=True) + EPS) * w.astype(F32)
    return y.astype(x.dtype)


def rope(x, pos):
    half = x.shape[-1] // 2
    inv = ROPE_THETA ** (-jnp.arange(half, dtype=F32) / half)
    ang = pos.astype(F32)[:, None] * inv[None, :]
    cos = jnp.cos(ang)[None, :, None, :]
    sin = jnp.sin(ang)[None, :, None, :]
    x1 = x[..., :half].astype(F32)
    x2 = x[..., half:].astype(F32)
    return jnp.concatenate([x1 * cos - x2 * sin, x2 * cos + x1 * sin], axis=-1).astype(x.dtype)


def ret_log_decay():
    return jnp.log1p(-jnp.exp2(-5.0 - jnp.arange(RET_HEADS, dtype=F32)))


def retention_chunk(S, qkv):
    q, k, v = qkv
    C = q.shape[1]
    lg = ret_log_decay()
    idx = jnp.arange(C, dtype=F32)
    diff = idx[:, None] - idx[None, :]
    causal = diff >= 0
    dmask = jnp.where(causal[None], jnp.exp(jnp.where(causal, diff, 0.0)[None] * lg[:, None, None]), 0.0)
    scores = jnp.einsum('bihd,bjhd->bhij', q, k) * dmask[None]
    o = jnp.einsum('bhij,bjhv->bihv', scores, v)
    o = o + jnp.einsum('bihd,bhdv->bihv', q, S) * jnp.exp((idx + 1.0)[:, None] * lg[None, :])[None, :, :, None]
    k_dec = k * jnp.exp((C - 1.0 - idx)[:, None] * lg[None, :])[None, :, :, None]
    S_new = S * jnp.exp(C * lg)[None, :, None, None] + jnp.einsum('bjhd,bjhv->bhdv', k_dec, v)
    return S_new, o


def retention_prompt(q, k, v):
    B, T = q.shape[:2]
    nc = T // RET_CHUNK
    def chunks(x):
        return jnp.moveaxis(x.astype(F32).reshape(B, nc, RET_CHUNK, x.shape[2], x.shape[3]), 1, 0)
    S0 = jnp.zeros((B, RET_HEADS, RET_DK, RET_DV), F32)
    S_fin, o = lax.scan(retention_chunk, S0, (chunks(q), chunks(k), chunks(v)))
    o = jnp.moveaxis(o, 0, 1).reshape(B, T, RET_HEADS, RET_DV)
    return S_fin, o


def sink_attention(q, kk, vv, qpos, kpos, sinks):
    s = jnp.einsum('bnqkgd,bnskd->bnkgqs', q.astype(F32), kk.astype(F32)) * (SWA_DH ** -0.5)
    rel = qpos[:, :, None] - kpos[:, None, :]
    mask = (rel >= 0) & (rel <= WINDOW) & (kpos[:, None, :] >= 0)
    s = jnp.where(mask[None, :, None, None], s, -jnp.inf)
    sink = sinks.astype(F32).reshape(1, 1, SWA_KV_HEADS, SWA_GROUP, 1, 1)
    m = jnp.maximum(jnp.max(s, axis=-1, keepdims=True), sink)
    p = jnp.exp(s - m)
    p = p / (jnp.sum(p, axis=-1, keepdims=True) + jnp.exp(sink - m))
    return jnp.einsum('bnkgqs,bnskd->bnqkgd', p, vv.astype(F32))


def swa_prompt(q, k, v, sinks):
    B, T = q.shape[:2]
    nb = T // SWA_BLOCK
    qb = q.reshape(B, nb, SWA_BLOCK, SWA_KV_HEADS, SWA_GROUP, SWA_DH)
    def band(x):
        xb = x.reshape(B, nb, SWA_BLOCK, SWA_KV_HEADS, SWA_DH)
        prev = jnp.concatenate([jnp.zeros_like(xb[:, :1]), xb[:, :-1]], axis=1)
        return jnp.concatenate([prev, xb], axis=2)
    qpos = jnp.arange(T, dtype=jnp.int32).reshape(nb, SWA_BLOCK)
    kpos = qpos[:, :1] - SWA_BLOCK + jnp.arange(2 * SWA_BLOCK, dtype=jnp.int32)[None, :]
    o = sink_attention(qb, band(k), band(v), qpos, kpos, sinks)
    return o.reshape(B, T, SWA_WIDTH)


def swa_sample(q, k, v, kbuf, vbuf, pos, sinks):
    B, T = q.shape[:2]
    kk = jnp.concatenate([kbuf.astype(k.dtype), k], axis=1)
    vv = jnp.concatenate([vbuf.astype(v.dtype), v], axis=1)
    qb = q.reshape(B, 1, T, SWA_KV_HEADS, SWA_GROUP, SWA_DH)
    kpos = jnp.concatenate([PAST_LEN - WINDOW + jnp.arange(WINDOW, dtype=jnp.int32), pos])[None, :]
    o = sink_attention(qb, kk[:, None], vv[:, None], pos[None, :], kpos, sinks)
    return o.reshape(B, T, SWA_WIDTH), kk[:, -WINDOW:], vv[:, -WINDOW:]


def mixer_inputs(h, pos, w_in, q_norm_w, k_norm_w):
    B, T, _ = h.shape
    proj = jnp.einsum('btd,de->bte', h, w_in)
    split_at = np.cumsum(PROJ_SIZES)[:-1].tolist()
    rq, rk, rv, rg, sq, sk, sv = jnp.split(proj, split_at, axis=-1)
    rq = rope(rq.reshape(B, T, RET_HEADS, RET_DK), pos)
    rk = rope(rk.reshape(B, T, RET_HEADS, RET_DK), pos) * (RET_DK ** -0.5)
    rv = rv.reshape(B, T, RET_HEADS, RET_DV)
    sq = rope(rmsnorm(sq.reshape(B, T, SWA_HEADS, SWA_DH), q_norm_w), pos)
    sk = rope(rmsnorm(sk.reshape(B, T, SWA_KV_HEADS, SWA_DH), k_norm_w), pos)
    sv = sv.reshape(B, T, SWA_KV_HEADS, SWA_DH)
    return rq, rk, rv, rg, sq, sk, sv


def mixer_output(ret_o, rg, swa_o, ret_gn_w, w_o, dtype):
    B, T = ret_o.shape[:2]
    mu = jnp.mean(ret_o, axis=-1, keepdims=True)
    var = jnp.mean(jnp.square(ret_o - mu), axis=-1, keepdims=True)
    on = ((ret_o - mu) * lax.rsqrt(var + EPS)).reshape(B, T, RET_WIDTH) * ret_gn_w.astype(F32)
    ret_out = jax.nn.silu(rg.astype(F32)) * on
    cat = jnp.concatenate([ret_out, swa_o], axis=-1).astype(dtype)
    return jnp.einsum('bte,ed->btd', cat, w_o)


def conv_ffn(h, conv_buf, ffn_norm_w, w_up, conv_w, conv_b, w_down):
    T = h.shape[1]
    hn = rmsnorm(h, ffn_norm_w)
    up = jnp.einsum('btd,df->btf', hn, w_up)
    g, u = jnp.split(up, 2, axis=-1)
    gp = jnp.concatenate([conv_buf.astype(g.dtype), g], axis=1)
    gc = conv_b + sum(gp[:, j:j + T] * conv_w[j] for j in range(CONV_W))
    y = jnp.einsum('btf,fd->btd', jax.nn.silu(gc) * u, w_down)
    return y, gp[:, -(CONV_W - 1):]


def setup_inputs(seed: int = 0) -> dict:
    key = jax.random.key(seed)
    ks = jax.random.split(key, 18)
    def nrm(k, shape, scale):
        return jax.random.normal(k, shape, F32) * scale
    return {
        'x_prompt': nrm(ks[0], (BATCH, SEQ, D_MODEL), 1.0),
        'x_sample': nrm(ks[1], (DEC_BATCH, DEC_SEQ, D_MODEL), 1.0),
        'state_ret': nrm(ks[2], (DEPTH, DEC_BATCH, RET_HEADS, RET_DK, RET_DV), 0.5),
        'cache_swa_k': nrm(ks[3], (DEPTH, DEC_BATCH, WINDOW, SWA_KV_HEADS, SWA_DH), 1.0),
        'cache_swa_v': nrm(ks[4], (DEPTH, DEC_BATCH, WINDOW, SWA_KV_HEADS, SWA_DH), 1.0),
        'state_conv': nrm(ks[5], (DEPTH, DEC_BATCH, CONV_W - 1, D_FF), 1.0),
        'attn_norm_w': 1.0 + nrm(ks[6], (DEPTH, D_MODEL), 0.02),
        'w_in': nrm(ks[7], (DEPTH, D_MODEL, PROJ_WIDTH), D_MODEL ** -0.5),
        'swa_q_norm_w': 1.0 + nrm(ks[8], (DEPTH, SWA_DH), 0.02),
        'swa_k_norm_w': 1.0 + nrm(ks[9], (DEPTH, SWA_DH), 0.02),
        'swa_sinks': nrm(ks[10], (DEPTH, SWA_HEADS), 1.0),
        'ret_gn_w': 1.0 + nrm(ks[11], (DEPTH, RET_WIDTH), 0.02),
        'w_o': nrm(ks[12], (DEPTH, MIX_WIDTH, D_MODEL), MIX_WIDTH ** -0.5),
        'ffn_norm_w': 1.0 + nrm(ks[13], (DEPTH, D_MODEL), 0.02),
        'w_up': nrm(ks[14], (DEPTH, D_MODEL, 2 * D_FF), D_MODEL ** -0.5),
        'conv_w': nrm(ks[15], (DEPTH, CONV_W, D_FF), CONV_W ** -0.5),
        'conv_b': nrm(ks[16], (DEPTH, D_FF), 0.01),
        'w_down': nrm(ks[17], (DEPTH, D_FF, D_MODEL), D_FF ** -0.5),
    }


def reference(x_prompt, x_sample, state_ret, cache_swa_k, cache_swa_v, state_conv,
              attn_norm_w, w_in, swa_q_norm_w, swa_k_norm_w, swa_sinks, ret_gn_w, w_o,
              ffn_norm_w, w_up, conv_w, conv_b, w_down):
    pos_p = jnp.arange(SEQ, dtype=jnp.int32)
    pos_s = PAST_LEN + jnp.arange(DEC_SEQ, dtype=jnp.int32)
    hp, hs = x_prompt, x_sample
    ret_p, ret_s, kp, ks_, vp, vs, cp, cs = [], [], [], [], [], [], [], []
    for l in range(DEPTH):
        a = rmsnorm(hp, attn_norm_w[l])
        rq, rk, rv, rg, sq, sk, sv = mixer_inputs(a, pos_p, w_in[l], swa_q_norm_w[l], swa_k_norm_w[l])
        S_p, ro = retention_prompt(rq, rk, rv)
        so = swa_prompt(sq, sk, sv, swa_sinks[l])
        hp = hp + mixer_output(ro, rg, so, ret_gn_w[l], w_o[l], hp.dtype)
        zbuf = jnp.zeros((hp.shape[0], CONV_W - 1, D_FF), hp.dtype)
        f, cbuf_p = conv_ffn(hp, zbuf, ffn_norm_w[l], w_up[l], conv_w[l], conv_b[l], w_down[l])
        hp = hp + f
        ret_p.append(S_p.astype(state_ret.dtype))
        kp.append(sk[:, -WINDOW:].astype(cache_swa_k.dtype))
        vp.append(sv[:, -WINDOW:].astype(cache_swa_v.dtype))
        cp.append(cbuf_p.astype(state_conv.dtype))
        a = rmsnorm(hs, attn_norm_w[l])
        rq, rk, rv, rg, sq, sk, sv = mixer_inputs(a, pos_s, w_in[l], swa_q_norm_w[l], swa_k_norm_w[l])
        S_s, ro = retention_chunk(state_ret[l].astype(F32), (rq.astype(F32), rk.astype(F32), rv.astype(F32)))
        so, kbuf_s, vbuf_s = swa_sample(sq, sk, sv, cache_swa_k[l], cache_swa_v[l], pos_s, swa_sinks[l])
        hs = hs + mixer_output(ro, rg, so, ret_gn_w[l], w_o[l], hs.dtype)
        f, cbuf_s = conv_ffn(hs, state_conv[l], ffn_norm_w[l], w_up[l], conv_w[l], conv_b[l], w_down[l])
        hs = hs + f
        ret_s.append(S_s.astype(state_ret.dtype))
        ks_.append(kbuf_s.astype(cache_swa_k.dtype))
        vs.append(vbuf_s.astype(cache_swa_v.dtype))
        cs.append(cbuf_s.astype(state_conv.dtype))
    return (hp, hs, jnp.stack(ret_p), jnp.stack(ret_s), jnp.stack(kp), jnp.stack(ks_),
            jnp.stack(vp), jnp.stack(vs), jnp.stack(cp), jnp.stack(cs))
```

```python
import contextlib
import numpy as np
import concourse.bass as bass
import concourse.mybir as mybir
from concourse.bass_utils import run_bass_kernel_spmd

F32 = mybir.dt.float32
BF16 = mybir.dt.bfloat16
AF = mybir.ActivationFunctionType
ALU = mybir.AluOpType
AX = mybir.AxisListType

D = 2048
DFF = 5632
NKC = 44
EPS = 1e-6
NCORES = 8
ENABLE_SAMPLE = True


class Buf:
    __slots__ = ("w", "r")

    def __init__(self):
        self.w = None
        self.r = {}


class Eng:
    def __init__(self, e, sem, name):
        self.e = e
        self.sem = sem
        self.name = name
        self.n = 0
        self.seen = {}
        self.dsems = []
        self.di = 0

    def wait(self, ev):
        if ev is None:
            return
        sem, val, key = ev
        if key == self.name and self.name == "pe":
            return
        if self.seen.get(key, 0) >= val:
            return
        self.e.wait_ge(sem, val)
        self.seen[key] = val

    def cur(self):
        return (self.sem, self.n, self.name)


def use(E, reads=(), writes=()):
    for b in reads:
        E.wait(b.w)
    for b in writes:
        E.wait(b.w)
        for ev in list(b.r.values()):
            E.wait(ev)


def record(ev, reads=(), writes=()):
    for b in reads:
        b.r[ev[2]] = ev
    for b in writes:
        b.w = ev
        b.r = {}


def fin(E, ins, reads=(), writes=()):
    E.n += 1
    ins.then_inc(E.sem, 1)
    record((E.sem, E.n, E.name), reads, writes)


def op(E, fn, reads=(), writes=()):
    use(E, reads, writes)
    ins = fn()
    fin(E, ins, reads, writes)
    return ins


NDS = 8


def dma(Q, out, in_, reads=(), writes=()):
    use(Q, reads, writes)
    i = Q.di
    Q.di += 1
    sem = Q.dsems[i % NDS]
    key = "%s_d%d" % (Q.name, i % NDS)
    val = 16 * (i // NDS + 1)
    if i >= NDS:
        Q.wait((sem, val - 16, key))
    Q.e.dma_start(out=out, in_=in_).then_inc(sem, 16)
    record((sem, val, key), reads, writes)


def build():
    nc = bass.Bass("TRN2", target_bir_lowering=False)

    def din(name, shape, dt=F32):
        return nc.dram_tensor(name, list(shape), dt, kind="ExternalInput").ap()

    def dout(name, shape, dt=F32):
        return nc.dram_tensor(name, list(shape), dt, kind="ExternalOutput").ap()

    def dint(name, shape, dt=BF16):
        return nc.dram_tensor(name, list(shape), dt, kind="Internal").ap()

    xc = din("xc", [4096, D])
    xs = din("xs", [64, D])
    sret = din("sret", [16, 8, 64, 128])
    ck = din("ck", [16, 128, 128])
    cv = din("cv", [16, 128, 128])
    sconv = din("sconv", [32, DFF])
    w_in = din("w_in", [D, 4352])
    w_o = din("w_o", [D, D])
    w_up = din("w_up", [D, 2 * DFF])
    w_down = din("w_down", [DFF, D])
    prm = din("prm", [208, 128])
    qkn = din("qkn", [1, 128])
    sinks = din("sinks", [1, 16])
    gnw = din("gnw", [1, 1024])
    rope = din("rope", [33, 128, 128])
    c_dmask = din("c_dmask", [128, 1024])
    c_gq = din("c_gq", [128, 512])
    c_kd = din("c_kd", [128, 8])
    c_gS = din("c_gS", [128, 4])
    c_mcur = din("c_mcur", [128, 128])
    c_mprev = din("c_mprev", [128, 128])
    c_mprev1 = din("c_mprev1", [128, 128])
    s_dmask = din("s_dmask", [64, 512])
    s_gq = din("s_gq", [128, 512])
    s_kd = din("s_kd", [64, 8])
    s_g4 = din("s_g4", [128, 8])
    s_sel2 = din("s_sel2", [128, 512])
    s_selk = din("s_selk", [64, 16])
    s_mc = din("s_mc", [128, 1024])
    s_mn = din("s_mn", [64, 64])

    yp = dout("yp", [2048, D])
    ys = dout("ys", [64, D])
    rsp = dout("rsp", [8, 64, 128])
    rss = dout("rss", [16, 8, 64, 128])
    kp = dout("kp", [128, 128])
    vp = dout("vp", [128, 128])
    ks = dout("ks", [16, 128, 128])
    vs = dout("vs", [16, 128, 128])
    cp = dout("cp", [2, DFF])
    cs = dout("cs", [32, DFF])

    wi = dint("wi", [9, 2, 128, 8, 512])
    wo = dint("wo", [4, 2, 128, 8, 512])
    wu = dint("wu", [2, 22, 128, 16, 256])
    wd = dint("wd", [4, 6, 128, 8, 512])

    with contextlib.ExitStack() as es:
        def sb(name, shape, dt=F32):
            return es.enter_context(nc.sbuf_tensor(name, list(shape), dt))

        def mksem(name):
            return es.enter_context(nc.semaphore(name))

        PE = Eng(nc.tensor, mksem("s_pe"), "pe")
        ACT = Eng(nc.scalar, mksem("s_act"), "act")
        DVE = Eng(nc.vector, mksem("s_dve"), "dve")
        POOL = Eng(nc.gpsimd, mksem("s_pool"), "pool")
        SP = Eng(nc.sync, mksem("s_sp"), "sp")
        for Q in (SP, POOL):
            Q.dsems = [mksem("%s_ds%d" % (Q.name, i)) for i in range(NDS)]
        ENGS = [PE, ACT, DVE, POOL, SP]

        def barrier():
            evs = [E.cur() for E in ENGS if E.n > 0]
            for Q in (SP, POOL):
                for i in range(min(Q.di, NDS)):
                    last = ((Q.di - 1 - i) // NDS) * NDS + i
                    cnt = (Q.di - 1 - i) // NDS + 1
                    evs.append((Q.dsems[i], 16 * cnt, "%s_d%d" % (Q.name, i)))
            for E in ENGS:
                for ev in evs:
                    if ev[2] == E.name:
                        continue
                    E.wait(ev)

        ps = es.enter_context(nc.psum_tensor("ps", [128, 8, 512], F32))
        PB = [Buf() for _ in range(8)]

        def bank(b, n=1):
            return ps[:, b:b + n, :].rearrange("p b c -> p (b c)")

        def bank16(b, n=1):
            return ps[:, b:b + n, :].rearrange("p b c -> p (b c)").bitcast(BF16)

        RING = 5
        wring = [sb("wr%d" % i, [128, 8, 512], BF16) for i in range(RING)]
        wringB = [Buf() for _ in range(RING)]
        wstate = {"i": 0}
        xT = sb("xT", [128, 16, 512], BF16)
        xTB = Buf()
        identF = sb("identF", [128, 128], F32)
        identB = sb("identB", [128, 128], BF16)
        onesF = sb("onesF", [128, 128], F32)
        colp = sb("colp", [128, 256], F32)
        qkn_sb = sb("qkn_sb", [128, 128], F32)
        esink = sb("esink", [128, 16], F32)
        gnw_sb = sb("gnw_sb", [128, 1024], F32)
        junk = sb("junk", [128, 1024], F32)
        junkB = Buf()
        rbc = sb("rbc", [128, 512], F32)
        rbcB = Buf()
        nst = sb("nst", [128, 16], F32)
        nstB = Buf()
        diag = [sb("diag%d" % i, [128, 128], F32) for i in range(2)]
        diagB = [Buf(), Buf()]
        tmpf = [sb("tmpf%d" % i, [128, 516], F32) for i in range(2)]
        tmpfB = [Buf(), Buf()]
        tmpg = [sb("tmpg%d" % i, [128, 512], F32) for i in range(2)]
        tmpgB = [Buf(), Buf()]
        tmph = [sb("tmph%d" % i, [128, 512], F32) for i in range(2)]
        tmphB = [Buf(), Buf()]
        sst = sb("sst", [128, 64], F32)
        sstB = Buf()
        CONST = Buf()
        tstate = {"i": 0}

        for idt in (identF, identB):
            op(POOL, lambda: nc.gpsimd.memset(idt[:], 0.0), (), (CONST,))
            op(POOL, lambda: nc.gpsimd.affine_select(out=idt[:], in_=idt[:], pattern=[[-1, 128]],
                                                     compare_op=ALU.not_equal, fill=1.0, base=0,
                                                     channel_multiplier=1), (CONST,), (CONST,))
        op(POOL, lambda: nc.gpsimd.memset(onesF[:], 1.0), (), (CONST,))

        WI = [[Buf() for _ in range(2)] for _ in range(9)]
        WO = [[Buf() for _ in range(2)] for _ in range(4)]
        WU = [[Buf() for _ in range(22)] for _ in range(2)]
        WD = [[Buf() for _ in range(6)] for _ in range(4)]
        for nb in range(9):
            cw = 512 if nb < 8 else 256
            for kt in range(2):
                src = w_in[kt * 1024:(kt + 1) * 1024, nb * 512:nb * 512 + cw].rearrange("(k p) c -> p k c", p=128)
                dma(POOL, wi[nb, kt, :, :, 0:cw], src, (), (WI[nb][kt],))
        prm_sb = sb("prm_sb", [128, 2, 128], F32)
        cbufs = []

        def cdma(out, in_):
            b_ = Buf()
            cbufs.append(b_)
            dma(SP, out, in_, (), (b_,))
            return b_

        def cjoin():
            op(ACT, lambda: nc.scalar.copy(out=nst[0:1, 15:16], in_=onesF[0:1, 0:1]), list(cbufs) + [CONST], (CONST, nstB))
            del cbufs[:]

        bp0 = cdma(prm_sb[:, 0, :], prm[0:128, :])
        bp1 = cdma(prm_sb[0:80, 1, :], prm[128:208, :])
        cdma(qkn_sb[:], qkn.partition_broadcast(128))
        bes = cdma(esink[:], sinks.partition_broadcast(128))
        cdma(gnw_sb[:], gnw.partition_broadcast(128))
        use(PE, (CONST, bp0, bp1), (PB[0],))
        nc.tensor.transpose(bank(0)[:, 0:128], prm_sb[:, 0, :], identF[:])
        ins = nc.tensor.transpose(bank(0)[:, 128:208], prm_sb[0:80, 1, :], identF[0:80, 0:80])
        fin(PE, ins, (CONST, bp0, bp1), (PB[0],))
        bcolp = Buf()
        cbufs.append(bcolp)
        op(ACT, lambda: nc.scalar.copy(out=colp[:, 0:208], in_=bank(0)[:, 0:208]), (PB[0],), (bcolp,))
        op(ACT, lambda: nc.scalar.activation(out=esink[:], in_=esink[:], func=AF.Exp), (bes,), (bes,))

        def cwcol(j, kc):
            return colp[:, j * 44 + kc:j * 44 + kc + 1]

        def cbcol(kc):
            return colp[:, 132 + kc:133 + kc]

        for nb in range(4):
            for kt in range(2):
                src = w_o[kt * 1024:(kt + 1) * 1024, nb * 512:(nb + 1) * 512].rearrange("(k p) c -> p k c", p=128)
                dma(POOL, wo[nb, kt], src, (), (WO[nb][kt],))
        for j in range(22):
            for gu in range(2):
                src = w_up[:, gu * DFF + j * 256:gu * DFF + (j + 1) * 256].rearrange("(k p) c -> p k c", p=128)
                dma(POOL, wu[gu, j], src, (), (WU[gu][j],))
        for nb in range(4):
            for kt in range(6):
                nk = 8 if kt < 5 else 4
                src = w_down[kt * 1024:kt * 1024 + nk * 128, nb * 512:(nb + 1) * 512].rearrange("(k p) c -> p k c", p=128)
                dma(POOL, wd[nb, kt, :, 0:nk, :], src, (), (WD[nb][kt],))

        def wload(src_ap, srcB, nk=8, cw=512):
            i = wstate["i"]
            wstate["i"] += 1
            s = i % RING
            dma(SP, wring[s][:, 0:nk, 0:cw], src_ap, (srcB,), (wringB[s],))
            return wring[s], wringB[s]

        def rsqrt_ip(ap, B):
            op(ACT, lambda: nc.scalar.activation(out=ap, in_=ap, func=AF.Ln), (B,), (B,))
            op(ACT, lambda: nc.scalar.activation(out=ap, in_=ap, func=AF.Exp, scale=-0.5), (B,), (B,))

        def phase_norm(slots, wofs):
            toff = 0
            offs = []
            for c, (h, hB, nt) in enumerate(slots):
                offs.append(toff)
                toff += nt
            T = toff
            for c, (h, hB, nt) in enumerate(slots):
                for hf in range(2):
                    op(ACT, lambda: nc.scalar.activation(out=junk[:nt, :], in_=h[:nt, hf * 1024:(hf + 1) * 1024],
                                                         func=AF.Square, accum_out=nst[:nt, 2 * c + hf:2 * c + hf + 1]),
                       (hB,), (junkB, nstB))
                op(DVE, lambda: nc.vector.tensor_tensor(out=nst[:nt, 8 + c:9 + c], in0=nst[:nt, 2 * c:2 * c + 1],
                                                        in1=nst[:nt, 2 * c + 1:2 * c + 2], op=ALU.add),
                   (nstB,), (nstB,))
                op(DVE, lambda: nc.vector.tensor_scalar(out=nst[:nt, 8 + c:9 + c], in0=nst[:nt, 8 + c:9 + c],
                                                        scalar1=1.0 / D, scalar2=EPS, op0=ALU.mult, op1=ALU.add),
                   (nstB,), (nstB,))
                rsqrt_ip(nst[:nt, 8 + c:9 + c], nstB)
                dg, dgB = diag[c % 2], diagB[c % 2]
                op(DVE, lambda: nc.vector.tensor_scalar(out=dg[:nt, :nt], in0=identF[:nt, :nt],
                                                        scalar1=nst[:nt, 8 + c:9 + c], scalar2=None, op0=ALU.mult),
                   (nstB, CONST), (dgB,))
                op(PE, lambda: nc.tensor.matmul(bank(7)[:, offs[c]:offs[c] + nt], lhsT=onesF[:nt, :], rhs=dg[:nt, :nt],
                                                start=True, stop=True), (dgB, CONST), (PB[7],))
            op(ACT, lambda: nc.scalar.copy(out=rbc[:, 0:T], in_=bank(7)[:, 0:T]), (PB[7],), (rbcB,))
            for k in range(16):
                b = k % 4
                use(PE, [s[1] for s in slots] + [CONST], (PB[b],))
                for c, (h, hB, nt) in enumerate(slots):
                    ins = nc.tensor.transpose(bank(b)[:, offs[c]:offs[c] + nt], h[:nt, k * 128:(k + 1) * 128],
                                              identF[:nt, :nt])
                fin(PE, ins, [s[1] for s in slots], (PB[b],))
                op(DVE, lambda: nc.vector.scalar_tensor_tensor(out=xT[:, k, 0:T], in0=bank(b)[:, 0:T],
                                                               scalar=colp[:, wofs + k:wofs + k + 1], in1=rbc[:, 0:T],
                                                               op0=ALU.mult, op1=ALU.mult),
                   (PB[b], rbcB, CONST), (xTB,))
            return offs, T

        def proj_tm(slots, offs, lhs_fn, lhsB, nkc, tiles_fn, ncol, evac, active=None, set0=0, hook=None):
            for nbi, nb in enumerate(ncol):
                base = 4 * ((nbi + set0) % 2)
                act_slots = [c for c in range(len(slots)) if active is None or active(nb, c)]
                tiles = tiles_fn(nb)
                k0 = 0
                for ti, (src, srcB, nk, cw) in enumerate(tiles):
                    wt, wtB = wload(src, srcB, nk, cw)
                    wr = [PB[base + c] for c in act_slots] if ti == 0 else []
                    use(PE, (wtB, lhsB), wr)
                    ins = None
                    for c in act_slots:
                        nt = slots[c][2]
                        for kk in range(nk):
                            ins = nc.tensor.matmul(bank(base + c)[:nt, 0:cw], lhsT=lhs_fn(c, k0 + kk),
                                                   rhs=wt[:, kk, 0:cw], start=(k0 + kk == 0),
                                                   stop=(k0 + kk == nkc - 1))
                    fin(PE, ins, (wtB, lhsB), [PB[base + c] for c in act_slots])
                    k0 += nk
                if hook is not None and nbi == 0:
                    hook()
                for c in act_slots:
                    evac(nb, c, bank(base + c), PB[base + c])

        pstate = {"on": False}

        def tt2(out, in0, in1, op_, reads, writes):
            if pstate["on"]:
                op(POOL, lambda: nc.gpsimd.tensor_tensor(out=out, in0=in0, in1=in1, op=op_), reads, writes)
            else:
                op(DVE, lambda: nc.vector.tensor_tensor(out=out, in0=in0, in1=in1, op=op_), reads, writes)

        def rope_ops_ap(src, srcB, dstap, dstB, H, nt, rtab_, rtB, scaled):
            o = 64 if scaled else 0
            C = rtab_[:nt, o:o + 32].unsqueeze(1).to_broadcast([nt, H, 32])
            Sn = rtab_[:nt, o + 32:o + 64].unsqueeze(1).to_broadcast([nt, H, 32])
            x = src[:nt, 0:H * 64].rearrange("p (h t d) -> p h t d", h=H, t=2)
            y = dstap.rearrange("p h (t d) -> p h t d", t=2)
            i = tstate["i"] % 2
            tstate["i"] += 1
            tg, tgB = tmpg[i], tmpgB[i]
            t = tg[:nt, 0:H * 64].rearrange("p (h t d) -> p h t d", h=H, t=2)
            op(DVE, lambda: nc.vector.tensor_tensor(out=t[:, :, 0, :], in0=x[:, :, 0, :], in1=C, op=ALU.mult),
               (srcB, rtB), (tgB,))
            op(DVE, lambda: nc.vector.tensor_tensor(out=t[:, :, 1, :], in0=x[:, :, 1, :], in1=Sn, op=ALU.mult),
               (srcB, rtB), (tgB,))
            op(DVE, lambda: nc.vector.tensor_tensor(out=y[:, :, 0, :], in0=t[:, :, 0, :], in1=t[:, :, 1, :],
                                                    op=ALU.subtract), (tgB,), (dstB,))
            th, thB = tmph[i], tmphB[i]
            u = th[:nt, 0:H * 64].rearrange("p (h t d) -> p h t d", h=H, t=2)
            tt2(u[:, :, 0, :], x[:, :, 1, :], C, ALU.mult, (srcB, rtB), (thB,))
            tt2(u[:, :, 1, :], x[:, :, 0, :], Sn, ALU.mult, (srcB, rtB), (thB,))
            tt2(y[:, :, 1, :], u[:, :, 0, :], u[:, :, 1, :], ALU.add, (thB,), (dstB,))

        def rope_ops(src, srcB, dst, dstB, H, nt, rtab_, rtB, scaled):
            rope_ops_ap(src, srcB, dst[:nt, 0:H * 64].rearrange("p (h d) -> p h d", h=H), dstB, H, nt, rtab_, rtB,
                        scaled)

        def headnorm(tf, tfB, H, nt, wcol0):
            op(ACT, lambda: nc.scalar.activation(out=junk[:nt, 0:H * 64], in_=tf[:nt, 0:H * 64], func=AF.Square),
               (tfB,), (junkB,))
            op(DVE, lambda: nc.vector.tensor_reduce(out=sst[:nt, 0:H],
                                                    in_=junk[:nt, 0:H * 64].rearrange("p (h d) -> p h d", h=H),
                                                    axis=AX.X, op=ALU.add), (junkB,), (sstB,))
            op(DVE, lambda: nc.vector.tensor_scalar(out=sst[:nt, 0:H], in0=sst[:nt, 0:H], scalar1=1.0 / 64,
                                                    scalar2=EPS, op0=ALU.mult, op1=ALU.add), (sstB,), (sstB,))
            rsqrt_ip(sst[:nt, 0:H], sstB)
            v = tf[:nt, 0:H * 64].rearrange("p (h d) -> p h d", h=H)
            op(DVE, lambda: nc.vector.tensor_tensor(out=v, in0=v, in1=sst[:nt, 0:H].unsqueeze(2).to_broadcast([nt, H, 64]),
                                                    op=ALU.mult), (tfB, sstB), (tfB,))
            op(DVE, lambda: nc.vector.tensor_tensor(out=v, in0=v,
                                                    in1=qkn_sb[:nt, wcol0:wcol0 + 64].unsqueeze(1).to_broadcast([nt, H, 64]),
                                                    op=ALU.mult), (tfB, CONST), (tfB,))

        with contextlib.ExitStack() as es2:
            def sb2(name, shape, dt=F32):
                return es2.enter_context(nc.sbuf_tensor(name, list(shape), dt))

            hs = [sb2("h%d" % i, [128, D], F32) for i in range(4)]
            hB = [Buf() for _ in range(4)]
            NS = 4
            big = sb2("big", [128, NKC * 512], BF16)
            SLOT = 4608
            rqb = [big[:, i * SLOT + 0:i * SLOT + 512] for i in range(NS)]
            kbb = [big[:, i * SLOT + 512:i * SLOT + 1024] for i in range(NS)]
            kdec = [big[:, i * SLOT + 1024:i * SLOT + 1536] for i in range(NS)]
            rvb = [big[:, i * SLOT + 1536:i * SLOT + 2560] for i in range(NS)]
            rgs = [big[:, i * SLOT + 2560:i * SLOT + 3584] for i in range(NS)]
            sqb = [big[:, i * SLOT + 3584:i * SLOT + 4608] for i in range(NS)]
            skf = [sb2("skf%d" % i, [128, 128], F32) for i in range(NS)]
            svf = [sb2("svf%d" % i, [128, 128], F32) for i in range(NS)]
            skb = [sb2("skb%d" % i, [128, 128], BF16) for i in range(NS)]
            slotB = [Buf() for _ in range(NS)]
            act = big[:, :].rearrange("p (k t) -> p k t", k=NKC)
            actB = Buf()
            guard = {"pe": None}
            rtab = [sb2("rtab%d" % i, [128, 128], F32) for i in range(NS)]
            rtabB = [Buf() for _ in range(NS)]
            dmask = sb2("dmask", [128, 1024], F32)
            gq = sb2("gq", [128, 512], F32)
            kd = sb2("kd", [128, 8], F32)
            gS = sb2("gS", [128, 4], F32)
            mcur = sb2("mcur", [128, 128], F32)
            mprev = sb2("mprev", [128, 128], F32)
            mprev1 = sb2("mprev1", [128, 128], F32)
            for t_, d_ in ((dmask, c_dmask), (gq, c_gq), (kd, c_kd), (gS, c_gS), (mcur, c_mcur), (mprev, c_mprev),
                           (mprev1, c_mprev1)):
                cdma(t_[:], d_)
            S = sb2("S", [128, 4, 128], F32)
            SB_ = Buf()
            Sb = [sb2("Sb%d" % i, [128, 4, 128], BF16) for i in range(2)]
            SbB = [Buf(), Buf()]
            skT = [sb2("skT%d" % i, [128, 128], BF16) for i in range(2)]
            svx = [sb2("svx%d" % i, [128, 2, 65], BF16) for i in range(2)]
            kvB = [Buf(), Buf()]
            qkT = sb2("qkT", [128, 8, 128], BF16)
            qkTB = Buf()
            qgT = sb2("qgT", [128, 4, 128], BF16)
            qgTB = Buf()
            sqT = sb2("sqT", [128, 8, 128], BF16)
            sqTB = Buf()
            scm = sb2("scm", [128, 1024], BF16)
            scmB = Buf()
            osb = sb2("osb", [128, 1024], F32)
            osbB = Buf()
            gst = sb2("gst", [128, 64], F32)
            gstB = Buf()
            pT = [sb2("pT%d" % i, [128, 1024], BF16) for i in range(2)]
            pTB = [Buf(), Buf()]
            cat0 = sb2("cat0", [128, D], BF16)
            cat = [cat0, cat0]
            catB0 = Buf()
            catB = [catB0, catB0]
            gcar = sb2("gcar", [128, 2, NKC], F32)
            gcarB = Buf()
            gsb, gsbB = tmpf, tmpfB
            cva, cvaB = tmpg, tmpgB

            op(DVE, lambda: nc.vector.memset(S[:], 0.0), (), (SB_,))
            op(DVE, lambda: nc.vector.memset(Sb[0][:], 0.0), (), (SbB[0],))
            op(DVE, lambda: nc.vector.memset(gcar[:], 0.0), (), (gcarB,))
            for i in range(2):
                op(DVE, lambda: nc.vector.memset(svx[i][:], 1.0), (), (kvB[i],))
                op(DVE, lambda: nc.vector.memset(skT[i][:], 0.0), (), (kvB[i],))
            st = {"sp": 0, "kv": 0, "cat": 0}

            def tiles_wi(nb):
                cw = 512 if nb < 8 else 256
                return [(wi[nb, kt, :, :, 0:cw], WI[nb][kt], 8, cw) for kt in range(2)]

            def tiles_wo(nb):
                return [(wo[nb, kt], WO[nb][kt], 8, 512) for kt in range(2)]

            def tiles_wd(nb):
                return [(wd[nb, kt, :, 0:(8 if kt < 5 else 4), :], WD[nb][kt], 8 if kt < 5 else 4, 512)
                        for kt in range(6)]

            def evac_in(slots, kinds):
                def ev(nb, c, pa, pB):
                    nt = slots[c][2]
                    sB = slotB[c]
                    if nb in (0, 1):
                        i = tstate["i"] % 2
                        tf, tfB = tmpf[i], tmpfB[i]
                        op(ACT, lambda: nc.scalar.copy(out=tf[:nt, 0:512], in_=pa[:nt, :]), (pB,), (tfB,))
                        if nb == 0:
                            rope_ops(tf, tfB, rqb[c], sB, 8, nt, rtab[c], rtabB[c], False)
                        else:
                            rope_ops(tf, tfB, kbb[c], sB, 8, nt, rtab[c], rtabB[c], True)
                            tt2(kdec[c][:nt, :].rearrange("p (h d) -> p h d", h=8),
                                kbb[c][:nt, :].rearrange("p (h d) -> p h d", h=8),
                                kd[:nt, :].unsqueeze(2).to_broadcast([nt, 8, 64]), ALU.mult, (sB, CONST), (sB,))
                    elif nb in (2, 3):
                        op(ACT, lambda: nc.scalar.copy(out=rvb[c][:nt, (nb - 2) * 512:(nb - 1) * 512], in_=pa[:nt, :]),
                           (pB,), (sB,))
                    elif nb in (4, 5):
                        op(ACT, lambda: nc.scalar.activation(out=rgs[c][:nt, (nb - 4) * 512:(nb - 3) * 512],
                                                             in_=pa[:nt, :], func=AF.Silu), (pB,), (sB,))
                    elif nb in (6, 7):
                        i = tstate["i"] % 2
                        tf, tfB = tmpf[i], tmpfB[i]
                        op(ACT, lambda: nc.scalar.copy(out=tf[:nt, 0:512], in_=pa[:nt, :]), (pB,), (tfB,))
                        headnorm(tf, tfB, 8, nt, 0)
                        rope_ops_ap(tf, tfB,
                                    sqb[c][:nt, :].rearrange("p (g k d) -> p g k d", g=8, k=2)[:, :, nb - 6, :],
                                    sB, 8, nt, rtab[c], rtabB[c], False)
                    else:
                        i = tstate["i"] % 2
                        tf, tfB = tmpf[i], tmpfB[i]
                        op(ACT, lambda: nc.scalar.copy(out=tf[:nt, 0:256], in_=pa[:nt, 0:256]), (pB,), (tfB,))
                        headnorm(tf, tfB, 2, nt, 64)
                        rope_ops(tf, tfB, skf[c], sB, 2, nt, rtab[c], rtabB[c], False)
                        op(ACT, lambda: nc.scalar.copy(out=skb[c][:nt, :], in_=skf[c][:nt, :]), (sB,), (sB,))
                        op(ACT, lambda: nc.scalar.copy(out=svf[c][:nt, :], in_=tf[:nt, 128:256]), (tfB,), (sB,))
                return ev

            def state_update(c):
                use(PE, (slotB[c],), (PB[6], PB[7]))
                for h in range(8):
                    ins = nc.tensor.matmul(bank(6 + h // 4)[:, (h % 4) * 128:(h % 4 + 1) * 128],
                                           lhsT=kdec[c][:, (h // 2) * 128:(h // 2 + 1) * 128],
                                           rhs=rvb[c][:, h * 128:(h + 1) * 128], start=True, stop=True)
                fin(PE, ins, (slotB[c],), (PB[6], PB[7]))
                tt2(S[:], S[:], gS[:, :].unsqueeze(2).to_broadcast([128, 4, 128]), ALU.mult, (SB_, CONST), (SB_,))
                for bk in range(2):
                    pv = bank(6 + bk).rearrange("p (b t v) -> p b t v", b=2, t=2)
                    op(DVE, lambda: nc.vector.tensor_tensor(out=S[0:64, 2 * bk:2 * bk + 2, :],
                                                            in0=S[0:64, 2 * bk:2 * bk + 2, :], in1=pv[0:64, :, 0, :],
                                                            op=ALU.add), (SB_, PB[6 + bk]), (SB_,))
                    op(DVE, lambda: nc.vector.tensor_tensor(out=S[64:128, 2 * bk:2 * bk + 2, :],
                                                            in0=S[64:128, 2 * bk:2 * bk + 2, :], in1=pv[64:128, :, 1, :],
                                                            op=ALU.add), (SB_, PB[6 + bk]), (SB_,))
                st["sp"] ^= 1
                p = st["sp"]
                op(ACT, lambda: nc.scalar.copy(out=Sb[p][:], in_=S[:]), (SB_,), (SbB[p],))

            def kv_carry(c):
                st["kv"] ^= 1
                p = st["kv"]
                op(PE, lambda: nc.tensor.transpose(bank16(7)[:, 0:128], skb[c][:, :], identB[:]),
                   (slotB[c], CONST), (PB[7],))
                op(ACT, lambda: nc.scalar.copy(out=skT[p][:], in_=bank16(7)[:, 0:128]), (PB[7],), (kvB[p],))
                op(ACT, lambda: nc.scalar.copy(out=svx[p][:, :, 0:64],
                                               in_=svf[c][:, :].rearrange("p (k d) -> p k d", k=2)),
                   (slotB[c],), (kvB[p],))

            def gn_gate(c, nt, rg_t, cat_t, catBuf, ob_lo=4):
                op(ACT, lambda: nc.scalar.copy(out=osb[:nt, :], in_=bank(ob_lo, 2)[:nt, :]),
                   (PB[ob_lo], PB[ob_lo + 1]), (osbB,))
                o3 = osb[:nt, :].rearrange("p (h v) -> p h v", h=8)
                op(DVE, lambda: nc.vector.tensor_reduce(out=gst[:nt, 0:8], in_=o3, axis=AX.X, op=ALU.add),
                   (osbB,), (gstB,))
                op(ACT, lambda: nc.scalar.activation(out=junk[:nt, 0:1024], in_=osb[:nt, :], func=AF.Square),
                   (osbB,), (junkB,))
                op(DVE, lambda: nc.vector.tensor_reduce(out=gst[:nt, 8:16],
                                                        in_=junk[:nt, 0:1024].rearrange("p (h v) -> p h v", h=8),
                                                        axis=AX.X, op=ALU.add), (junkB,), (gstB,))
                op(DVE, lambda: nc.vector.tensor_scalar(out=gst[:nt, 0:8], in0=gst[:nt, 0:8], scalar1=1.0 / 128,
                                                        scalar2=None, op0=ALU.mult), (gstB,), (gstB,))
                op(DVE, lambda: nc.vector.tensor_tensor(out=gst[:nt, 16:24], in0=gst[:nt, 0:8], in1=gst[:nt, 0:8],
                                                        op=ALU.mult), (gstB,), (gstB,))
                op(DVE, lambda: nc.vector.scalar_tensor_tensor(out=gst[:nt, 8:16], in0=gst[:nt, 8:16],
                                                               scalar=1.0 / 128, in1=gst[:nt, 16:24],
                                                               op0=ALU.mult, op1=ALU.subtract), (gstB,), (gstB,))
                op(DVE, lambda: nc.vector.tensor_scalar(out=gst[:nt, 8:16], in0=gst[:nt, 8:16], scalar1=EPS,
                                                        scalar2=None, op0=ALU.add), (gstB,), (gstB,))
                rsqrt_ip(gst[:nt, 8:16], gstB)
                op(DVE, lambda: nc.vector.tensor_tensor(out=o3, in0=o3,
                                                        in1=gst[:nt, 0:8].unsqueeze(2).to_broadcast([nt, 8, 128]),
                                                        op=ALU.subtract), (osbB, gstB), (osbB,))
                op(DVE, lambda: nc.vector.tensor_tensor(out=o3, in0=o3,
                                                        in1=gst[:nt, 8:16].unsqueeze(2).to_broadcast([nt, 8, 128]),
                                                        op=ALU.mult), (osbB, gstB), (osbB,))
                op(DVE, lambda: nc.vector.tensor_tensor(out=osb[:nt, :], in0=osb[:nt, :], in1=gnw_sb[:nt, :],
                                                        op=ALU.mult), (osbB, CONST), (osbB,))
                op(DVE, lambda: nc.vector.tensor_tensor(out=cat_t[:nt, 0:1024], in0=osb[:nt, :], in1=rg_t[:nt, :],
                                                        op=ALU.mult), (osbB, slotB[c]), (catBuf,))

            def swa_norm(nt, cat_t, catBuf, ob=4):
                for seg, (h0, n) in enumerate(((0, 7), (7, 7), (14, 2))):
                    v = bank(ob + seg)[:nt, 0:n * 65].rearrange("p (h e) -> p h e", e=65)
                    op(DVE, lambda: nc.vector.tensor_tensor(out=gst[:nt, 32 + h0:32 + h0 + n], in0=v[:, :, 64],
                                                            in1=esink[:nt, h0:h0 + n], op=ALU.add),
                       (PB[ob + seg], CONST), (gstB,))
                    op(DVE, lambda: nc.vector.reciprocal(out=gst[:nt, 32 + h0:32 + h0 + n],
                                                         in_=gst[:nt, 32 + h0:32 + h0 + n]), (gstB,), (gstB,))
                    op(DVE, lambda: nc.vector.tensor_tensor(
                        out=cat_t[:nt, 1024 + h0 * 64:1024 + (h0 + n) * 64].rearrange("p (h d) -> p h d", d=64),
                        in0=v[:, :, 0:64],
                        in1=gst[:nt, 32 + h0:32 + h0 + n].unsqueeze(2).to_broadcast([nt, n, 64]), op=ALU.mult),
                       (PB[ob + seg], gstB), (catBuf,))

            def cat_to_xT(nt, cat_t, catBuf, off):
                for half in range(2):
                    b = half
                    use(PE, (catBuf, CONST), (PB[b],))
                    for k in range(8):
                        ins = nc.tensor.transpose(bank16(b)[:, k * 128:k * 128 + nt],
                                                  cat_t[:nt, (half * 8 + k) * 128:(half * 8 + k + 1) * 128],
                                                  identB[:nt, :nt])
                    fin(PE, ins, (catBuf,), (PB[b],))
                    op(ACT, lambda: nc.scalar.copy(
                        out=xT[:, half * 8:half * 8 + 8, off:off + nt],
                        in_=bank16(b)[:, 0:1024].rearrange("p (k t) -> p k t", k=8)[:, :, 0:nt]),
                       (PB[b],), (xTB,))

            def mixer_full(c, off, first_prev_mask):
                import os as _os
                KMIX = int(_os.environ.get("KMIX", "99"))
                sB = slotB[c]
                use(PE, (sB, CONST), (PB[0],))
                for b in range(4):
                    nc.tensor.transpose(bank16(0)[:, b * 128:(b + 1) * 128], rqb[c][:, b * 128:(b + 1) * 128], identB[:])
                for b in range(4):
                    ins = nc.tensor.transpose(bank16(0)[:, (4 + b) * 128:(5 + b) * 128],
                                              kbb[c][:, b * 128:(b + 1) * 128], identB[:])
                fin(PE, ins, (sB,), (PB[0],))
                op(ACT, lambda: nc.scalar.copy(out=qkT[:].rearrange("p b t -> p (b t)"), in_=bank16(0)[:, 0:1024]),
                   (PB[0],), (qkTB,))
                op(DVE, lambda: nc.vector.tensor_tensor(out=qgT[:].rearrange("p b t -> p (b t)"),
                                                        in0=qkT[:, 0:4, :].rearrange("p b t -> p (b t)"),
                                                        in1=gq[:, :], op=ALU.mult),
                   (qkTB, CONST), (qgTB,))
                use(PE, (sB, CONST), (PB[1],))
                for g in range(8):
                    ins = nc.tensor.transpose(bank16(1)[:, g * 128:(g + 1) * 128], sqb[c][:, g * 128:(g + 1) * 128],
                                              identB[:])
                fin(PE, ins, (sB,), (PB[1],))
                op(ACT, lambda: nc.scalar.copy(out=sqT[:].rearrange("p b t -> p (b t)"), in_=bank16(1)[:, 0:1024]),
                   (PB[1],), (sqTB,))
                if KMIX <= 1:
                    return
                use(PE, (qkTB,), (PB[2], PB[3]))
                for h in range(8):
                    hp = (h % 2) * 64
                    ins = nc.tensor.matmul(bank(2 + h % 2)[:, (h // 2) * 128:(h // 2 + 1) * 128],
                                           lhsT=qkT[hp:hp + 64, 4 + h // 2, :], rhs=qkT[hp:hp + 64, h // 2, :],
                                           start=True, stop=True)
                fin(PE, ins, (qkTB,), (PB[2], PB[3]))
                for i in range(2):
                    op(DVE, lambda: nc.vector.tensor_tensor(out=scm[:, i * 512:(i + 1) * 512], in0=bank(2 + i),
                                                            in1=dmask[:, i * 512:(i + 1) * 512], op=ALU.mult),
                       (PB[2 + i], CONST), (scmB,))
                if KMIX <= 2:
                    return
                p = st["sp"]
                use(PE, (scmB, sB, qgTB, SbB[p]), (PB[4], PB[5]))
                for h in range(8):
                    hp = (h % 2) * 64
                    o_ap = bank(4 + h // 4)[:, (h % 4) * 128:(h % 4 + 1) * 128]
                    hb = (h % 2) * 4 + h // 2
                    nc.tensor.matmul(o_ap, lhsT=scm[:, hb * 128:(hb + 1) * 128], rhs=rvb[c][:, h * 128:(h + 1) * 128],
                                     start=True, stop=False)
                    ins = nc.tensor.matmul(o_ap, lhsT=qgT[hp:hp + 64, h // 2, :], rhs=Sb[p][hp:hp + 64, h // 2, :],
                                           start=False, stop=True)
                fin(PE, ins, (scmB, sB, qgTB, SbB[p]), (PB[4], PB[5]))
                if KMIX <= 3:
                    return
                state_update(c)
                if KMIX <= 4:
                    return
                st["cat"] ^= 1
                ct, ctB = cat[st["cat"]], catB[st["cat"]]
                gn_gate(c, 128, rgs[c], ct, ctB)
                if KMIX <= 5:
                    return
                pprev = st["kv"]
                kv_carry(c)
                if KMIX <= 6:
                    return
                pcur = st["kv"]
                for kvh in range(2):
                    kp0 = kvh * 64
                    pars = (pprev, pcur)
                    for blk in range(2):
                        par = pars[blk]
                        mk = (first_prev_mask if first_prev_mask is not None else mprev) if blk == 0 else mcur
                        bb = 2 * blk
                        use(PE, (kvB[par], sqTB), (PB[bb], PB[bb + 1]))
                        nc.tensor.matmul(bank(bb), lhsT=skT[par][kp0:kp0 + 64, :],
                                         rhs=sqT[kp0:kp0 + 64, 0:4, :], start=True, stop=True)
                        ins = nc.tensor.matmul(bank(bb + 1), lhsT=skT[par][kp0:kp0 + 64, :],
                                               rhs=sqT[kp0:kp0 + 64, 4:8, :], start=True, stop=True)
                        fin(PE, ins, (kvB[par], sqTB), (PB[bb], PB[bb + 1]))
                        pt, ptB = pT[blk], pTB[blk]
                        op(ACT, lambda: nc.scalar.activation(out=pt[:, :], in_=bank(bb, 2), func=AF.Exp, scale=0.125),
                           (PB[bb], PB[bb + 1]), (ptB,))
                        tt2(pt[:, :].rearrange("p (g t) -> p g t", g=8), pt[:, :].rearrange("p (g t) -> p g t", g=8),
                            mk[:, :].unsqueeze(1).to_broadcast([128, 8, 128]), ALU.mult, (ptB, CONST), (ptB,))
                    wr = (PB[4], PB[5], PB[6]) if kvh == 0 else ()
                    rd = (pTB[0], pTB[1], kvB[pprev], kvB[pcur])
                    use(PE, rd, wr)
                    for g in range(8):
                        hq = kvh * 8 + g
                        o_ap = bank(4 + hq // 7)[:, (hq % 7) * 65:(hq % 7) * 65 + 65]
                        nc.tensor.matmul(o_ap, lhsT=pT[0][:, g * 128:(g + 1) * 128], rhs=svx[pprev][:, kvh, :],
                                         start=True, stop=False)
                        ins = nc.tensor.matmul(o_ap, lhsT=pT[1][:, g * 128:(g + 1) * 128], rhs=svx[pcur][:, kvh, :],
                                               start=False, stop=True)
                    fin(PE, ins, rd, (PB[4], PB[5], PB[6]))
                if KMIX <= 7:
                    return
                swa_norm(128, ct, ctB)
                if KMIX <= 8:
                    return
                cat_to_xT(128, ct, ctB, off)

            def phase_up(T, segs, g_only=False):
                if g_only:
                    for j in range(22):
                        tl = [wload(wu[0, j, :, kt * 8:(kt + 1) * 8, :], WU[0][j], 8, 256) for kt in range(2)]
                        banks = [2 * (j % 4), 2 * (j % 4) + 1]
                        for kt in range(2):
                            wt, wtB = tl[kt]
                            wr = [PB[banks[0]], PB[banks[1]]] if kt == 0 else []
                            use(PE, (wtB, xTB), wr)
                            for f in range(2):
                                for kk in range(8):
                                    k = kt * 8 + kk
                                    ins = nc.tensor.matmul(bank(banks[f])[:, 0:T], lhsT=wt[:, kk, f * 128:(f + 1) * 128],
                                                           rhs=xT[:, k, 0:T], start=(k == 0), stop=(k == 15))
                            fin(PE, ins, (wtB, xTB), [PB[banks[0]], PB[banks[1]]])
                        for f in range(2):
                            kc = 2 * j + f
                            op(DVE, lambda: nc.vector.tensor_copy(out=gcar[:, :, kc], in_=bank(banks[f])[:, T - 2:T]),
                               (PB[banks[f]],), (gcarB,))
                    return
                for j in range(22):
                    tl = []
                    for gu in range(2):
                        for kt in range(2):
                            tl.append(wload(wu[gu, j, :, kt * 8:(kt + 1) * 8, :], WU[gu][j], 8, 256))
                    s = j % 2
                    banks = [4 * s, 4 * s + 1, 4 * s + 2, 4 * s + 3]
                    for gu in range(2):
                        for kt in range(2):
                            wt, wtB = tl[gu * 2 + kt]
                            wr = [PB[banks[gu * 2]], PB[banks[gu * 2 + 1]]] if kt == 0 else []
                            use(PE, (wtB, xTB), wr)
                            for f in range(2):
                                for kk in range(8):
                                    k = kt * 8 + kk
                                    ins = nc.tensor.matmul(bank(banks[gu * 2 + f])[:, 0:T],
                                                           lhsT=wt[:, kk, f * 128:(f + 1) * 128], rhs=xT[:, k, 0:T],
                                                           start=(k == 0), stop=(k == 15))
                            fin(PE, ins, (wtB, xTB), [PB[banks[gu * 2]], PB[banks[gu * 2 + 1]]])
                    for f in range(2):
                        kc = 2 * j + f
                        bg, bu = banks[f], banks[2 + f]
                        gi = kc % 2
                        g_, gB = gsb[gi], gsbB[gi]
                        a_, aB = cva[gi], cvaB[gi]
                        op(ACT, lambda: nc.scalar.copy(out=g_[:, 2:2 + T], in_=bank(bg)[:, 0:T]), (PB[bg],), (gB,))
                        op(ACT, lambda: nc.scalar.activation(out=a_[:, 0:T], in_=bank(bg)[:, 0:T], func=AF.Identity,
                                                             scale=cwcol(2, kc), bias=cbcol(kc)),
                           (PB[bg], CONST), (aB,))
                        for (off, n) in segs:
                            op(DVE, lambda: nc.vector.tensor_copy(out=g_[:, off:off + 2], in_=gcar[:, :, kc]),
                               (gcarB,), (gB,))
                            op(DVE, lambda: nc.vector.tensor_copy(out=gcar[:, :, kc], in_=g_[:, off + n:off + n + 2]),
                               (gB,), (gcarB,))
                            op(DVE, lambda: nc.vector.scalar_tensor_tensor(out=a_[:, off:off + n], in0=g_[:, off + 1:off + 1 + n],
                                                                           scalar=cwcol(1, kc), in1=a_[:, off:off + n],
                                                                           op0=ALU.mult, op1=ALU.add),
                               (gB, aB, CONST), (aB,))
                            op(DVE, lambda: nc.vector.scalar_tensor_tensor(out=a_[:, off:off + n], in0=g_[:, off:off + n],
                                                                           scalar=cwcol(0, kc), in1=a_[:, off:off + n],
                                                                           op0=ALU.mult, op1=ALU.add),
                               (gB, aB, CONST), (aB,))
                        op(ACT, lambda: nc.scalar.activation(out=a_[:, 0:T], in_=a_[:, 0:T], func=AF.Silu), (aB,), (aB,))
                        op(DVE, lambda: nc.vector.tensor_tensor(out=act[:, kc, 0:T], in0=a_[:, 0:T], in1=bank(bu)[:, 0:T],
                                                                op=ALU.mult), (aB, PB[bu]), (actB,))

            preloaded = set()

            def preload(next_cis, c):
                if next_cis is None or c >= len(next_cis) or next_cis[c] in preloaded:
                    return
                preloaded.add(next_cis[c])
                ci_ = next_cis[c]
                dma(SP, hs[c][:, :], xc[ci_ * 128:(ci_ + 1) * 128, :], (), (hB[c],))

            def run_group(cis, light, tail_only=False, next_cis=None):
                slots = [(hs[c], hB[c], 128) for c in range(len(cis))]
                for c, ci in enumerate(cis):
                    if ci not in preloaded:
                        dma(SP, hs[c][:, :], xc[ci * 128:(ci + 1) * 128, :], (), (hB[c],))
                    dma(SP, rtab[c][:, :], rope[ci], (), (rtabB[c],))
                for c in range(len(cis), 4):
                    preload(next_cis, c)
                offs, T = phase_norm(slots, 176)

                def light_hook():
                    for c in range(len(cis)):
                        preload(next_cis, c)
                if guard["pe"] is not None:
                    ACT.wait(guard["pe"])
                    DVE.wait(guard["pe"])
                ev = evac_in(slots, None)
                if light:
                    has14 = 14 in cis
                    cols = [1, 2, 3] + ([8] if has14 else [])
                    proj_tm(slots, offs, lambda c, k: xT[:, k, offs[c]:offs[c] + 128], xTB, 16, tiles_wi, cols, ev,
                            active=lambda nb, c: (nb != 8) or cis[c] == 14, hook=light_hook)
                    for c, ci in enumerate(cis):
                        state_update(c)
                        if ci == 14:
                            kv_carry(c)
                    return
                import os as _os
                KSUB = int(_os.environ.get("KSUB", "99"))
                KCOLS = int(_os.environ.get("KCOLS", "9"))
                proj_tm(slots, offs, lambda c, k: xT[:, k, offs[c]:offs[c] + 128], xTB, 16, tiles_wi,
                        [0, 2, 1, 3, 6, 4, 7, 5, 8][:KCOLS], ev)
                if KSUB <= 1:
                    return
                KMIXC = int(_os.environ.get("KMIXC", "99"))
                for c, ci in enumerate(cis):
                    if int(_os.environ.get("KMIXS", "0")) <= c < KMIXC:
                        mixer_full(c, offs[c], mprev1 if ci == 16 else None)
                if KSUB <= 2:
                    return

                def ev_o(nb, c, pa, pB):
                    op(DVE, lambda: nc.vector.tensor_tensor(out=hs[c][:, nb * 512:(nb + 1) * 512], in0=pa[:, :],
                                                            in1=hs[c][:, nb * 512:(nb + 1) * 512], op=ALU.add),
                       (pB, hB[c]), (hB[c],))
                proj_tm(slots, offs, lambda c, k: xT[:, k, offs[c]:offs[c] + 128], xTB, 16, tiles_wo, list(range(4)),
                        ev_o, set0=1)
                if KSUB <= 3:
                    return
                offs, T = phase_norm(slots, 192)
                for E_ in (PE, ACT, DVE):
                    DVE.wait(E_.cur())
                if tail_only:
                    for c in range(len(cis)):
                        preload(next_cis, c)
                phase_up(T, [(0, T)], g_only=tail_only)
                if KSUB <= 5 or tail_only:
                    return
                if 31 in cis:
                    c = cis.index(31)
                    op(PE, lambda: nc.tensor.transpose(bank(0)[0:88, 0:128],
                                                       gcar[:, :, :].rearrange("p t k -> p (t k)"), identF[:]),
                       (gcarB, CONST), (PB[0],))
                    op(ACT, lambda: nc.scalar.copy(out=junk[0:88, 0:128], in_=bank(0)[0:88, 0:128]), (PB[0],), (junkB,))
                    for t in range(2):
                        dma(SP, cp[t].rearrange("(k p) -> k p", p=128), junk[t * 44:(t + 1) * 44, 0:128], (junkB,), ())
                    dma(SP, kp[:, :], skf[c][:, :], (slotB[c],), ())
                    dma(SP, vp[:, :], svf[c][:, :], (slotB[c],), ())
                    for hp in range(2):
                        dma(SP, rsp.rearrange("(b hp) d v -> hp d b v", hp=2)[hp], S[hp * 64:(hp + 1) * 64, :, :],
                            (SB_,), ())

                def ev_d(nb, c, pa, pB):
                    op(DVE, lambda: nc.vector.tensor_tensor(out=hs[c][:, nb * 512:(nb + 1) * 512], in0=pa[:, :],
                                                            in1=hs[c][:, nb * 512:(nb + 1) * 512], op=ALU.add),
                       (pB, hB[c]), (hB[c],))
                    if nb == 3:
                        ci = cis[c]
                        dma(SP, yp[(ci - 16) * 128:(ci - 15) * 128, :], hs[c][:, :], (hB[c],), ())
                        preload(next_cis, c)
                proj_tm(slots, offs, lambda c, k: act[:, k, offs[c]:offs[c] + 128], actB, NKC, tiles_wd,
                        list(range(4)), ev_d, active=lambda nb, c: cis[c] >= 16)
                guard["pe"] = PE.cur()

            import os as _os
            KSTOP = int(_os.environ.get("KSTOP", "99"))
            glist = [([ci for ci in range(g0, min(g0 + 4, 15))], True) for g0 in (0, 4, 8, 12)]
            glist += [([15], False)]
            glist += [(list(range(g0, g0 + 4)), False) for g0 in (16, 20, 24, 28)]
            for c_, ci_ in enumerate(glist[0][0]):
                preloaded.add(ci_)
                dma(SP, hs[c_][:, :], xc[ci_ * 128:(ci_ + 1) * 128, :], (), (hB[c_],))
            cjoin()
            for gi, (cis_, light_) in enumerate(glist):
                if gi >= KSTOP:
                    break
                pstate["on"] = (not light_) and cis_[0] >= 20
                run_group(cis_, light_, tail_only=(cis_ == [15]),
                          next_cis=(glist[gi + 1][0] if gi + 1 < len(glist) else None))
            barrier()

        import os as _os2
        if ENABLE_SAMPLE and _os2.environ.get("KSAMPLE", "1") == "1":
            with contextlib.ExitStack() as es3:
                def sb3(name, shape, dt=F32):
                    return es3.enter_context(nc.sbuf_tensor(name, list(shape), dt))

                NT = 64
                pstate["on"] = True
                hS = sb3("hS", [128, D], F32)
                hSB = Buf()
                rqb = [sb3("rqb_s", [128, 512], BF16)]
                kbb = [sb3("kbb_s", [128, 512], BF16)]
                kdec = [sb3("kdec_s", [128, 512], BF16)]
                rvb = [sb3("rvb_s", [128, 1024], BF16)]
                rgs = [sb3("rgs_s", [128, 1024], BF16)]
                sqb = [sb3("sqb_s", [128, 1024], BF16)]
                skf = [sb3("skf_s", [128, 128], F32)]
                svf = [sb3("svf_s", [128, 128], F32)]
                skb = [sb3("skb_s", [128, 128], BF16)]
                slotB = [Buf()]
                rtab = [sb3("rtab_s", [128, 128], F32)]
                rtabB = [Buf()]
                osb = sb3("osb_s", [128, 1024], F32)
                osbB = Buf()
                gst = sb3("gst_s", [128, 64], F32)
                gstB = Buf()
                kd = sb3("kd_s", [128, 8], F32)
                t_dmask = sb3("t_dmask", [128, 512], F32)
                t_gq = sb3("t_gq", [128, 512], F32)
                t_g4 = sb3("t_g4", [128, 8], F32)
                t_sel2 = sb3("t_sel2", [128, 512], F32)
                t_selk = sb3("t_selk", [128, 16], F32)
                t_mc = sb3("t_mc", [128, 1024], F32)
                t_mn = sb3("t_mn", [128, 64], F32)
                cdma(kd[0:64, :], s_kd)
                cdma(t_dmask[0:64, :], s_dmask)
                cdma(t_gq[:, :], s_gq)
                cdma(t_g4[:, :], s_g4)
                cdma(t_sel2[:, :], s_sel2)
                cdma(t_selk[0:64, :], s_selk)
                cdma(t_mc[:, :], s_mc)
                cdma(t_mn[0:64, :], s_mn)
                cjoin()
                cat_s = sb3("cat_s", [128, D], BF16)
                catSB = Buf()
                act_s = sb3("act_s", [128, NKC, 64], BF16)
                actSB = Buf()
                kcT = sb3("kcT", [128, 16, 128], BF16)
                kcTB = Buf()
                vcx = sb3("vcx", [128, 16, 2, 65], BF16)
                vcxB = Buf()
                skT_s = sb3("skT_s", [128, 64], BF16)
                svx_s = sb3("svx_s", [128, 2, 65], BF16)
                kvSB = Buf()
                qkT_s = sb3("qkT_s", [128, 8, 64], BF16)
                qkTSB = Buf()
                sqT_s = sb3("sqT_s", [128, 8, 64], BF16)
                sqTSB = Buf()
                scm_s = sb3("scm_s", [128, 512], BF16)
                scmSB = Buf()
                rq2 = sb3("rq2", [128, 8, 2, 64], BF16)
                rq2B = Buf()
                qg_sb = sb3("qg_sb", [128, 512], F32)
                qgSB = Buf()
                qx = sb3("qx", [128, 4, 8, 64], BF16)
                qxB = Buf()
                kx = sb3("kx", [128, 4, 16, 64], BF16)
                kxB = Buf()
                Sin = sb3("Sin", [128, 8, 4, 128], F32)
                SinB = Buf()
                Sb16 = sb3("Sb16", [128, 8, 4, 128], BF16)
                Sb16B = Buf()
                pTa = sb3("pTa", [128, 17, 256], BF16)
                pTaB = Buf()
                kvc = pTa[:, :, :].rearrange("p a b -> p (a b)")[:, 0:4096].bitcast(F32).rearrange("p (b c) -> p b c", b=16)
                kvcB = pTaB
                scq0 = sb3("scq0", [32, 1408], F32)
                scq = [scq0, scq0]
                scqB0 = Buf()
                scqB = [scqB0, scqB0]
                gcS = sb3("gcS", [128, NKC, 32], F32)
                gcSB = Buf()
                gtl, gtlB = gcS, gcSB
                gs6 = [sb3("gs6_%d" % i, [128, 16, 6], F32) for i in range(2)]
                gs6B = [Buf(), Buf()]

                slots = [(hS, hSB, NT)]
                dma(SP, hS[0:NT, :], xs, (), (hSB,))
                dma(SP, rtab[0][:, :], rope[32], (), (rtabB[0],))
                dma(SP, ks[:, 0:124, :], ck[:, 4:128, :], (), ())
                dma(SP, vs[:, 0:124, :], cv[:, 4:128, :], (), ())
                dma(SP, kvc[:, :, :], ck.rearrange("b s c -> s b c"), (), (kvcB,))
                for q4 in range(4):
                    use(PE, (kvcB, CONST), (PB[q4],))
                    for i in range(4):
                        ins = nc.tensor.transpose(bank(q4)[:, i * 128:(i + 1) * 128], kvc[:, q4 * 4 + i, :], identF[:])
                    fin(PE, ins, (kvcB,), (PB[q4],))
                    op(ACT, lambda: nc.scalar.copy(out=kcT[:, q4 * 4:q4 * 4 + 4, :].rearrange("p b s -> p (b s)"),
                                                   in_=bank(q4)[:, :]), (PB[q4],), (kcTB,))
                dma(SP, kvc[:, :, :], cv.rearrange("b s c -> s b c"), (), (kvcB,))
                op(DVE, lambda: nc.vector.memset(vcx[:], 1.0), (), (vcxB,))
                op(DVE, lambda: nc.vector.tensor_copy(out=vcx[:, :, :, 0:64],
                                                      in_=kvc[:, :, :].rearrange("p b (k d) -> p b k d", k=2)),
                   (kvcB,), (vcxB,))
                op(DVE, lambda: nc.vector.memset(svx_s[:], 1.0), (), (kvSB,))

                offs, T = phase_norm(slots, 176)
                ev = evac_in(slots, None)
                proj_tm(slots, offs, lambda c, k: xT[:, k, 0:NT], xTB, 16, tiles_wi, [0, 2, 1, 3, 6, 4, 7, 5, 8], ev)
                sB = slotB[0]
                use(PE, (sB, CONST), (PB[0],))
                for b in range(4):
                    nc.tensor.transpose(bank16(0)[:, b * 64:(b + 1) * 64], rqb[0][0:NT, b * 128:(b + 1) * 128],
                                        identB[0:NT, 0:NT])
                for b in range(4):
                    ins = nc.tensor.transpose(bank16(0)[:, (4 + b) * 64:(5 + b) * 64],
                                              kbb[0][0:NT, b * 128:(b + 1) * 128], identB[0:NT, 0:NT])
                fin(PE, ins, (sB,), (PB[0],))
                op(ACT, lambda: nc.scalar.copy(out=qkT_s[:].rearrange("p b t -> p (b t)"), in_=bank16(0)[:, 0:512]),
                   (PB[0],), (qkTSB,))
                use(PE, (qkTSB,), (PB[2], PB[3]))
                for h in range(8):
                    hp = (h % 2) * 64
                    ins = nc.tensor.matmul(bank(2 + h % 2)[0:NT, (h // 2) * 64:(h // 2 + 1) * 64],
                                           lhsT=qkT_s[hp:hp + 64, 4 + h // 2, :],
                                           rhs=qkT_s[hp:hp + 64, h // 2, :], start=True, stop=True)
                fin(PE, ins, (qkTSB,), (PB[2], PB[3]))
                for i in range(2):
                    op(DVE, lambda: nc.vector.tensor_tensor(out=scm_s[0:NT, i * 256:(i + 1) * 256],
                                                            in0=bank(2 + i)[0:NT, 0:256],
                                                            in1=t_dmask[0:NT, i * 256:(i + 1) * 256],
                                                            op=ALU.mult), (PB[2 + i], CONST), (scmSB,))
                op(DVE, lambda: nc.vector.tensor_copy(
                    out=rq2[0:NT, :, :, :],
                    in_=rqb[0][0:NT, :].rearrange("p (h d) -> p h d", h=8).unsqueeze(2).to_broadcast([NT, 8, 2, 64])),
                   (sB,), (rq2B,))
                use(PE, (rq2B, CONST), (PB[3],))
                for h in range(8):
                    ins = nc.tensor.transpose(bank16(3)[:, h * 64:(h + 1) * 64],
                                              rq2[0:NT, h, :, :].rearrange("p a d -> p (a d)"), identB[0:NT, 0:NT])
                fin(PE, ins, (rq2B,), (PB[3],))
                op(DVE, lambda: nc.vector.tensor_tensor(out=qg_sb[:, :], in0=bank16(3)[:, 0:512], in1=t_gq[:, :],
                                                        op=ALU.mult), (PB[3], CONST), (qgSB,))
                for hh in range(2):
                    for p2 in range(2):
                        for b2 in range(8):
                            dma(SP, Sin[p2 * 64:(p2 + 1) * 64, b2, :, :],
                                sret[2 * b2 + p2, hh * 4:(hh + 1) * 4, :, :].rearrange("h d v -> d h v"),
                                (), (SinB,))
                    for b2 in range(8):
                        op(ACT, lambda: nc.scalar.copy(out=Sb16[:, b2, :, :], in_=Sin[:, b2, :, :]), (SinB,), (Sb16B,))
                    for hl in range(4):
                        h = hh * 4 + hl
                        op(DVE, lambda: nc.vector.tensor_tensor(
                            out=qx[:, hl, :, :], in0=qg_sb[:, h * 64:(h + 1) * 64].unsqueeze(1).to_broadcast([128, 8, 64]),
                            in1=t_sel2[:, :].rearrange("p (b t) -> p b t", b=8), op=ALU.mult),
                           (qgSB, CONST), (qxB,))
                        op(DVE, lambda: nc.vector.tensor_tensor(
                            out=kx[0:NT, hl, :, :],
                            in0=kdec[0][0:NT, h * 64:(h + 1) * 64].unsqueeze(1).to_broadcast([NT, 16, 64]),
                            in1=t_selk[0:NT, :].unsqueeze(2).to_broadcast([NT, 16, 64]), op=ALU.mult),
                           (sB, CONST), (kxB,))
                    use(PE, (scmSB, sB, qxB, Sb16B), (PB[4 + hh],))
                    for hl in range(4):
                        h = hh * 4 + hl
                        o_ap = bank(4 + hh)[0:NT, hl * 128:(hl + 1) * 128]
                        hb = (h % 2) * 4 + h // 2
                        nc.tensor.matmul(o_ap, lhsT=scm_s[0:NT, hb * 64:(hb + 1) * 64],
                                         rhs=rvb[0][0:NT, h * 128:(h + 1) * 128], start=True, stop=False)
                        for b2 in range(8):
                            ins = nc.tensor.matmul(o_ap, lhsT=qx[:, hl, b2, :], rhs=Sb16[:, b2, hl, :],
                                                   start=False, stop=(b2 == 7))
                    fin(PE, ins, (scmSB, sB, qxB, Sb16B), (PB[4 + hh],))
                    for b2 in range(8):
                        bk = b2 % 2
                        use(PE, (kxB, sB), (PB[bk],))
                        for hl in range(4):
                            h = hh * 4 + hl
                            ins = nc.tensor.matmul(bank(bk)[:, hl * 128:(hl + 1) * 128],
                                                   lhsT=kx[0:NT, hl, 2 * b2:2 * b2 + 2, :].rearrange("p a d -> p (a d)"),
                                                   rhs=rvb[0][0:NT, h * 128:(h + 1) * 128], start=True, stop=True)
                        fin(PE, ins, (kxB, sB), (PB[bk],))
                        sv_ = Sin[:, b2, :, :]
                        op(DVE, lambda: nc.vector.tensor_tensor(
                            out=sv_, in0=sv_, in1=t_g4[:, hh * 4:(hh + 1) * 4].unsqueeze(2).to_broadcast([128, 4, 128]),
                            op=ALU.mult), (SinB, CONST), (SinB,))
                        op(DVE, lambda: nc.vector.tensor_tensor(
                            out=sv_, in0=sv_, in1=bank(bk)[:, :].rearrange("p (h v) -> p h v", h=4), op=ALU.add),
                           (SinB, PB[bk]), (SinB,))
                    for p2 in range(2):
                        for b2 in range(8):
                            dma(SP, rss[2 * b2 + p2, hh * 4:(hh + 1) * 4, :, :].rearrange("h d v -> d h v"),
                                Sin[p2 * 64:(p2 + 1) * 64, b2, :, :], (SinB,), ())
                gn_gate(0, NT, rgs[0], cat_s, catSB)
                use(PE, (sB, CONST), (PB[1],))
                for g in range(8):
                    ins = nc.tensor.transpose(bank16(1)[:, g * 64:(g + 1) * 64], sqb[0][0:NT, g * 128:(g + 1) * 128],
                                              identB[0:NT, 0:NT])
                fin(PE, ins, (sB,), (PB[1],))
                op(ACT, lambda: nc.scalar.copy(out=sqT_s[:].rearrange("p b t -> p (b t)"), in_=bank16(1)[:, 0:512]),
                   (PB[1],), (sqTSB,))
                op(PE, lambda: nc.tensor.transpose(bank16(7)[:, 0:64], skb[0][0:NT, :], identB[0:NT, 0:NT]),
                   (sB, CONST), (PB[7],))
                op(ACT, lambda: nc.scalar.copy(out=skT_s[:, :], in_=bank16(7)[:, 0:64]), (PB[7],), (kvSB,))
                op(ACT, lambda: nc.scalar.copy(out=svx_s[0:NT, :, 0:64],
                                               in_=svf[0][0:NT, :].rearrange("p (k d) -> p k d", k=2)),
                   (sB,), (kvSB,))
                bi = 0
                for kvh in range(2):
                    kp0 = kvh * 64
                    for gh in range(2):
                        for kb in range(17):
                            nk = 128 if kb < 16 else NT
                            bb = bi % 4
                            bi += 1
                            lhs = kcT[kp0:kp0 + 64, kb, :] if kb < 16 else skT_s[kp0:kp0 + 64, :]
                            op(PE, lambda: nc.tensor.matmul(bank(bb)[0:nk, 0:256], lhsT=lhs,
                                                            rhs=sqT_s[kp0:kp0 + 64, gh * 4:gh * 4 + 4, :],
                                                            start=True, stop=True),
                               (kcTB, kvSB, sqTSB), (PB[bb],))
                            op(ACT, lambda: nc.scalar.activation(out=pTa[0:nk, kb, :], in_=bank(bb)[0:nk, 0:256],
                                                                 func=AF.Exp, scale=0.125), (PB[bb],), (pTaB,))
                            mk = (t_mc[:, kb * 64:(kb + 1) * 64] if kb < 16 else t_mn[0:NT, :])
                            op(POOL, lambda: nc.gpsimd.tensor_tensor(
                                out=pTa[0:nk, kb, :].rearrange("p (g t) -> p g t", g=4),
                                in0=pTa[0:nk, kb, :].rearrange("p (g t) -> p g t", g=4),
                                in1=mk.unsqueeze(1).to_broadcast([nk, 4, 64]), op=ALU.mult),
                               (pTaB, CONST), (pTaB,))
                        wr = (PB[4], PB[5], PB[6]) if (kvh == 0 and gh == 0) else ()
                        use(PE, (pTaB, vcxB, kvSB), wr)
                        for gl in range(4):
                            hq = kvh * 8 + gh * 4 + gl
                            o_ap = bank(4 + hq // 7)[0:NT, (hq % 7) * 65:(hq % 7) * 65 + 65]
                            for kb in range(17):
                                nk = 128 if kb < 16 else NT
                                rhs = vcx[:, kb, kvh, :] if kb < 16 else svx_s[0:NT, kvh, :]
                                ins = nc.tensor.matmul(o_ap, lhsT=pTa[0:nk, kb, gl * 64:(gl + 1) * 64], rhs=rhs,
                                                       start=(kb == 0), stop=(kb == 16))
                        fin(PE, ins, (pTaB, vcxB, kvSB), (PB[4], PB[5], PB[6]))
                swa_norm(NT, cat_s, catSB)
                cat_to_xT(NT, cat_s, catSB, 0)
                for b in range(16):
                    dma(SP, ks[b, 124:128, :], skf[0][4 * b:4 * b + 4, :], (sB,), ())
                    dma(SP, vs[b, 124:128, :], svf[0][4 * b:4 * b + 4, :], (sB,), ())

                def ev_os(nb, c, pa, pB):
                    op(DVE, lambda: nc.vector.tensor_tensor(out=hS[0:NT, nb * 512:(nb + 1) * 512], in0=pa[0:NT, :],
                                                            in1=hS[0:NT, nb * 512:(nb + 1) * 512], op=ALU.add),
                       (pB, hSB), (hSB,))
                proj_tm(slots, offs, lambda c, k: xT[:, k, 0:NT], xTB, 16, tiles_wo, list(range(4)), ev_os, set0=1)
                offs, T = phase_norm(slots, 192)
                for q4 in range(4):
                    sc_, scB_ = scq[q4 % 2], scqB[q4 % 2]
                    dma(SP, sc_[:, :], sconv[:, q4 * 1408:(q4 + 1) * 1408], (), (scB_,))
                    bk = q4 % 2
                    use(PE, (scB_, CONST), (PB[bk],))
                    for i in range(11):
                        ins = nc.tensor.transpose(bank(bk)[:, i * 32:(i + 1) * 32], sc_[:, i * 128:(i + 1) * 128],
                                                  identF[0:32, 0:32])
                    fin(PE, ins, (scB_,), (PB[bk],))
                    op(ACT, lambda: nc.scalar.copy(out=gcS[:, q4 * 11:(q4 + 1) * 11, :].rearrange("p k t -> p (k t)"),
                                                   in_=bank(bk)[:, 0:352]), (PB[bk],), (gcSB,))
                for j in range(22):
                    tl = []
                    for gu in range(2):
                        for kt in range(2):
                            tl.append(wload(wu[gu, j, :, kt * 8:(kt + 1) * 8, :], WU[gu][j], 8, 256))
                    s_ = j % 2
                    banks = [4 * s_, 4 * s_ + 1, 4 * s_ + 2, 4 * s_ + 3]
                    for gu in range(2):
                        for kt in range(2):
                            wt, wtB = tl[gu * 2 + kt]
                            wr = [PB[banks[gu * 2]], PB[banks[gu * 2 + 1]]] if kt == 0 else []
                            use(PE, (wtB, xTB), wr)
                            for f in range(2):
                                for kk in range(8):
                                    k = kt * 8 + kk
                                    ins = nc.tensor.matmul(bank(banks[gu * 2 + f])[:, 0:NT],
                                                           lhsT=wt[:, kk, f * 128:(f + 1) * 128], rhs=xT[:, k, 0:NT],
                                                           start=(k == 0), stop=(k == 15))
                            fin(PE, ins, (wtB, xTB), [PB[banks[gu * 2]], PB[banks[gu * 2 + 1]]])
                    for f in range(2):
                        kc = 2 * j + f
                        bg, bu = banks[f], banks[2 + f]
                        gi = kc % 2
                        g6, g6B = gs6[gi], gs6B[gi]
                        a_, aB = cva[gi], cvaB[gi]
                        gps = bank(bg)[:, 0:NT].rearrange("p (b t) -> p b t", t=4)
                        a3 = a_[:, 0:NT].rearrange("p (b t) -> p b t", t=4)
                        op(ACT, lambda: nc.scalar.copy(out=g6[:, :, 2:6], in_=gps), (PB[bg],), (g6B,))
                        op(ACT, lambda: nc.scalar.activation(out=a_[:, 0:NT], in_=bank(bg)[:, 0:NT], func=AF.Identity,
                                                             scale=cwcol(2, kc), bias=cbcol(kc)), (PB[bg], CONST), (aB,))
                        op(DVE, lambda: nc.vector.tensor_copy(out=g6[:, :, 0:2],
                                                              in_=gcS[:, kc, :].rearrange("p (b t) -> p b t", t=2)),
                           (gcSB,), (g6B,))
                        op(DVE, lambda: nc.vector.tensor_copy(out=gtl[:, kc, :].rearrange("p (b t) -> p b t", t=2),
                                                              in_=g6[:, :, 4:6]), (g6B,), (gtlB,))
                        op(DVE, lambda: nc.vector.scalar_tensor_tensor(out=a3, in0=g6[:, :, 1:5], scalar=cwcol(1, kc),
                                                                       in1=a3, op0=ALU.mult, op1=ALU.add),
                           (g6B, aB, CONST), (aB,))
                        op(DVE, lambda: nc.vector.scalar_tensor_tensor(out=a3, in0=g6[:, :, 0:4], scalar=cwcol(0, kc),
                                                                       in1=a3, op0=ALU.mult, op1=ALU.add),
                           (g6B, aB, CONST), (aB,))
                        op(ACT, lambda: nc.scalar.activation(out=a_[:, 0:NT], in_=a_[:, 0:NT], func=AF.Silu), (aB,), (aB,))
                        op(DVE, lambda: nc.vector.tensor_tensor(out=act_s[:, kc, :], in0=a_[:, 0:NT], in1=bank(bu)[:, 0:NT],
                                                                op=ALU.mult), (aB, PB[bu]), (actSB,))
                for q4 in range(4):
                    sc_, scB_ = scq[q4 % 2], scqB[q4 % 2]
                    for t3 in range(3):
                        k0 = q4 * 11 + t3 * 4
                        nk4 = 4 if t3 < 2 else 3
                        bk = (q4 * 3 + t3) % 2
                        use(PE, (gtlB, CONST), (PB[bk],))
                        for i in range(nk4):
                            ins = nc.tensor.transpose(bank(bk)[0:32, i * 128:(i + 1) * 128], gtl[:, k0 + i, :], identF[:])
                        fin(PE, ins, (gtlB,), (PB[bk],))
                        op(ACT, lambda: nc.scalar.copy(out=sc_[:, t3 * 512:t3 * 512 + nk4 * 128],
                                                       in_=bank(bk)[0:32, 0:nk4 * 128]), (PB[bk],), (scB_,))
                    dma(SP, cs[:, q4 * 1408:(q4 + 1) * 1408], sc_[:, :], (scB_,), ())

                def ev_ds(nb, c, pa, pB):
                    op(DVE, lambda: nc.vector.tensor_tensor(out=hS[0:NT, nb * 512:(nb + 1) * 512], in0=pa[0:NT, :],
                                                            in1=hS[0:NT, nb * 512:(nb + 1) * 512], op=ALU.add),
                       (pB, hSB), (hSB,))
                    if nb == 3:
                        dma(SP, ys[:, :], hS[0:NT, :], (hSB,), ())
                proj_tm(slots, offs, lambda c, k: act_s[:, k, :], actSB, NKC, tiles_wd, list(range(4)), ev_ds)
                barrier()
        barrier()
    return nc


def _tables():
    f = np.float32
    lg = np.log1p(-np.exp2(-5.0 - np.arange(8, dtype=f))).astype(f)
    i = np.arange(128, dtype=f)
    diff = i[None, :] - i[:, None]
    dm = np.where(diff[:, None, :] >= 0, np.exp(np.maximum(diff, 0)[:, None, :] * lg[None, :, None]), 0.0)
    c_dmask = np.ascontiguousarray(dm[:, [0, 2, 4, 6, 1, 3, 5, 7], :]).reshape(128, 1024).astype(f)
    p = np.arange(128)
    gq = np.zeros((128, 4, 128), f)
    for b in range(4):
        h = 2 * b + p // 64
        gq[:, b, :] = np.exp((i[None, :] + 1.0) * lg[h][:, None])
    c_gq = gq.reshape(128, 512)
    c_kd = np.exp((127.0 - i)[:, None] * lg[None, :]).astype(f)
    gS = np.zeros((128, 4), f)
    for b in range(4):
        gS[:, b] = np.exp(128.0 * lg[2 * b + p // 64])
    mcur = (i[:, None] <= i[None, :]).astype(f)
    mprev = (i[:, None] >= i[None, :]).astype(f)
    return dict(c_dmask=c_dmask, c_gq=c_gq, c_kd=c_kd, c_gS=gS, c_mcur=mcur, c_mprev=mprev), lg


def _rope_tab(pos):
    f = np.float32
    half = 32
    inv = (f(10000.0) ** (-(np.arange(half, dtype=f)) / f(half))).astype(f)
    ang = (pos.astype(f)[:, None] * inv[None, :]).astype(f)
    c = np.cos(ang).astype(f)
    s = np.sin(ang).astype(f)
    return np.concatenate([c, s, c * f(0.125), s * f(0.125)], axis=1).astype(f)


def _sample_tables(lg):
    f = np.float32
    tok = np.arange(64)
    b = tok // 4
    t = (tok % 4).astype(f)
    same = (b[:, None] == b[None, :])
    dm = np.zeros((64, 8, 64), f)
    for h in range(8):
        d = t[None, :] - t[:, None]
        dm[:, h, :] = np.where(same & (d >= 0), np.exp(np.maximum(d, 0) * lg[h]), 0.0)
    gq = np.zeros((128, 8, 64), f)
    for h in range(8):
        gq[:, h, :] = np.exp((t + 1.0) * lg[h])[None, :]
    kd = np.exp((3.0 - t)[:, None] * lg[None, :]).astype(f)
    g4 = np.tile(np.exp(4.0 * lg)[None, :], (128, 1)).astype(f)
    sel2 = np.zeros((128, 8, 64), f)
    for p2 in range(2):
        for b2 in range(8):
            sel2[p2 * 64:(p2 + 1) * 64, b2, :] = (b == 2 * b2 + p2).astype(f)[None, :]
    selk = (b[:, None] == np.arange(16)[None, :]).astype(f)
    s_ = np.arange(128)
    mc = np.zeros((128, 16, 64), f)
    for bb in range(16):
        mc[:, bb, :] = ((b[None, :] == bb) & (s_[:, None] >= (tok % 4)[None, :])).astype(f)
    mn = (same & ((tok % 4)[:, None] <= (tok % 4)[None, :])).astype(f)
    dm = np.ascontiguousarray(dm[:, [0, 2, 4, 6, 1, 3, 5, 7], :])
    return dict(s_dmask=dm.reshape(64, 512), s_gq=gq.reshape(128, 512), s_kd=kd, s_g4=g4,
                s_sel2=sel2.reshape(128, 512), s_selk=selk, s_mc=mc.reshape(128, 1024), s_mn=mn)


_NC_CACHE = {}


def kernel(x_prompt, x_sample, state_ret, cache_swa_k, cache_swa_v, state_conv,
           attn_norm_w, w_in, swa_q_norm_w, swa_k_norm_w, swa_sinks, ret_gn_w, w_o,
           ffn_norm_w, w_up, conv_w, conv_b, w_down):
    f = np.float32
    A = lambda a: np.ascontiguousarray(np.asarray(a), dtype=f)
    x_prompt, x_sample = A(x_prompt), A(x_sample)
    state_ret, cache_swa_k, cache_swa_v, state_conv = A(state_ret), A(cache_swa_k), A(cache_swa_v), A(state_conv)
    w_in2, w_o2, w_up2, w_down2 = A(w_in)[0], A(w_o)[0], A(w_up)[0], A(w_down)[0]
    prm = np.concatenate([A(conv_w)[0].reshape(132, 128), A(conv_b)[0].reshape(44, 128),
                          A(attn_norm_w)[0].reshape(16, 128), A(ffn_norm_w)[0].reshape(16, 128)], axis=0)
    qkn = np.concatenate([A(swa_q_norm_w)[0], A(swa_k_norm_w)[0]])[None, :]
    sinks = A(swa_sinks)[0][None, :]
    gnw = A(ret_gn_w)[0][None, :]
    ct, lg = _tables()
    stb = _sample_tables(lg)
    if "nc" not in _NC_CACHE:
        _NC_CACHE["nc"] = build()
    nc = _NC_CACHE["nc"]
    in_maps = []
    for c in range(NCORES):
        b, half = c // 2, c % 2
        if half == 0:
            xcv = np.concatenate([np.zeros((2048, D), f), x_prompt[b, :2048]], axis=0)
        else:
            xcv = x_prompt[b]
        pos0 = half * 2048 - 2048
        rp = np.zeros((33, 128, 128), f)
        for ci in range(32):
            pos = pos0 + ci * 128 + np.arange(128)
            rp[ci] = _rope_tab(np.maximum(pos, 0))
        rp[32, :64] = _rope_tab(16384 + (np.arange(64) % 4))
        m = dict(
            xc=np.ascontiguousarray(xcv), xs=np.ascontiguousarray(x_sample[c * 16:(c + 1) * 16].reshape(64, D)),
            sret=np.ascontiguousarray(state_ret[0, c * 16:(c + 1) * 16]),
            ck=np.ascontiguousarray(cache_swa_k[0, c * 16:(c + 1) * 16].reshape(16, 128, 128)),
            cv=np.ascontiguousarray(cache_swa_v[0, c * 16:(c + 1) * 16].reshape(16, 128, 128)),
            sconv=np.ascontiguousarray(state_conv[0, c * 16:(c + 1) * 16].reshape(32, DFF)),
            w_in=w_in2, w_o=w_o2, w_up=w_up2, w_down=w_down2, prm=prm, qkn=qkn, sinks=sinks, gnw=gnw, rope=rp,
            c_mprev1=(ct["c_mprev"] if half == 1 else np.zeros((128, 128), f)),
        )
        m.update(ct)
        m.update(stb)
        in_maps.append(m)
    res = run_bass_kernel_spmd(nc, in_maps, core_ids=list(range(NCORES)))
    R = res.results
    y_prompt = np.stack([np.concatenate([R[2 * b]["yp"], R[2 * b + 1]["yp"]], axis=0) for b in range(4)])
    y_sample = np.concatenate([R[c]["ys"].reshape(16, 4, D) for c in range(NCORES)], axis=0)
    ret_p = np.stack([R[2 * b + 1]["rsp"] for b in range(4)])[None]
    ret_s = np.concatenate([R[c]["rss"] for c in range(NCORES)], axis=0)[None]
    k_p = np.stack([R[2 * b + 1]["kp"].reshape(128, 2, 64) for b in range(4)])[None]
    v_p = np.stack([R[2 * b + 1]["vp"].reshape(128, 2, 64) for b in range(4)])[None]
    k_s = np.concatenate([R[c]["ks"].reshape(16, 128, 2, 64) for c in range(NCORES)], axis=0)[None]
    v_s = np.concatenate([R[c]["vs"].reshape(16, 128, 2, 64) for c in range(NCORES)], axis=0)[None]
    c_p = np.stack([R[2 * b + 1]["cp"] for b in range(4)])[None]
    c_s = np.concatenate([R[c]["cs"].reshape(16, 2, DFF) for c in range(NCORES)], axis=0)[None]
    out = (y_prompt, y_sample, ret_p, ret_s, k_p, k_s, v_p, v_s, c_p, c_s)
    return tuple(np.ascontiguousarray(o, dtype=f) for o in out)
```

```python
import contextlib
import numpy as np
import concourse.bass as bass
import concourse.mybir as mybir
from concourse.bass_utils import run_bass_kernel_spmd

F32 = mybir.dt.float32
BF16 = mybir.dt.bfloat16
AF = mybir.ActivationFunctionType
ALU = mybir.AluOpType
AX = mybir.AxisListType

D = 2048
DFF = 5632
NKC = 44
EPS = 1e-6
NCORES = 8
ENABLE_SAMPLE = True


class Buf:
    __slots__ = ("w", "r")

    def __init__(self):
        self.w = None
        self.r = {}


class Eng:
    def __init__(self, e, sem, name):
        self.e = e
        self.sem = sem
        self.name = name
        self.n = 0
        self.seen = {}
        self.dsems = []
        self.di = 0

    def wait(self, ev):
        if ev is None:
            return
        sem, val, key = ev
        if key == self.name and self.name == "pe":
            return
        if self.seen.get(key, 0) >= val:
            return
        self.e.wait_ge(sem, val)
        self.seen[key] = val

    def cur(self):
        return (self.sem, self.n, self.name)


def use(E, reads=(), writes=()):
    for b in reads:
        E.wait(b.w)
    for b in writes:
        E.wait(b.w)
        for ev in list(b.r.values()):
            E.wait(ev)


def record(ev, reads=(), writes=()):
    for b in reads:
        b.r[ev[2]] = ev
    for b in writes:
        b.w = ev
        b.r = {}


def fin(E, ins, reads=(), writes=()):
    E.n += 1
    ins.then_inc(E.sem, 1)
    record((E.sem, E.n, E.name), reads, writes)


def op(E, fn, reads=(), writes=()):
    use(E, reads, writes)
    ins = fn()
    fin(E, ins, reads, writes)
    return ins


NDS = 16


def dma(Q, out, in_, reads=(), writes=()):
    use(Q, reads, writes)
    i = Q.di
    Q.di += 1
    sem = Q.dsems[i % NDS]
    key = "%s_d%d" % (Q.name, i % NDS)
    val = 16 * (i // NDS + 1)
    if i >= NDS:
        Q.wait((sem, val - 16, key))
    Q.e.dma_start(out=out, in_=in_).then_inc(sem, 16)
    record((sem, val, key), reads, writes)


def build():
    nc = bass.Bass("TRN2", target_bir_lowering=False)

    def din(name, shape, dt=F32):
        return nc.dram_tensor(name, list(shape), dt, kind="ExternalInput").ap()

    def dout(name, shape, dt=F32):
        return nc.dram_tensor(name, list(shape), dt, kind="ExternalOutput").ap()

    def dint(name, shape, dt=BF16):
        return nc.dram_tensor(name, list(shape), dt, kind="Internal").ap()

    xc = din("xc", [4096, D])
    xs = din("xs", [64, D])
    sret = din("sret", [16, 8, 64, 128])
    ck = din("ck", [16, 128, 128])
    cv = din("cv", [16, 128, 128])
    sconv = din("sconv", [32, DFF])
    w_in = din("w_in", [D, 4352])
    w_o = din("w_o", [D, D])
    w_up = din("w_up", [D, 2 * DFF])
    w_down = din("w_down", [DFF, D])
    prm = din("prm", [208, 128])
    qkn = din("qkn", [1, 128])
    sinks = din("sinks", [1, 16])
    gnw = din("gnw", [1, 1024])
    rope = din("rope", [33, 128, 128])
    c_dmask = din("c_dmask", [128, 1024])
    c_gq = din("c_gq", [128, 512])
    c_kd = din("c_kd", [128, 8])
    c_gS = din("c_gS", [128, 4])
    c_mcur = din("c_mcur", [128, 128])
    c_mprev = din("c_mprev", [128, 128])
    c_mprev1 = din("c_mprev1", [128, 128])
    s_dmask = din("s_dmask", [64, 512])
    s_gq = din("s_gq", [128, 512])
    s_kd = din("s_kd", [64, 8])
    s_g4 = din("s_g4", [128, 8])
    s_sel2 = din("s_sel2", [128, 512])
    s_selk = din("s_selk", [64, 16])
    s_mc = din("s_mc", [128, 1024])
    s_mn = din("s_mn", [64, 64])

    yp = dout("yp", [2048, D])
    ys = dout("ys", [64, D])
    rsp = dout("rsp", [8, 64, 128])
    rss = dout("rss", [16, 8, 64, 128])
    kp = dout("kp", [128, 128])
    vp = dout("vp", [128, 128])
    ks = dout("ks", [16, 128, 128])
    vs = dout("vs", [16, 128, 128])
    cp = dout("cp", [2, DFF])
    cs = dout("cs", [32, DFF])

    wi = dint("wi", [9, 2, 128, 8, 512])
    wo = dint("wo", [4, 2, 128, 8, 512])
    wu = dint("wu", [2, 22, 128, 16, 256])
    wd = dint("wd", [4, 6, 128, 8, 512])

    with contextlib.ExitStack() as es:
        def sb(name, shape, dt=F32):
            return es.enter_context(nc.sbuf_tensor(name, list(shape), dt))

        def mksem(name):
            return es.enter_context(nc.semaphore(name))

        PE = Eng(nc.tensor, mksem("s_pe"), "pe")
        ACT = Eng(nc.scalar, mksem("s_act"), "act")
        DVE = Eng(nc.vector, mksem("s_dve"), "dve")
        POOL = Eng(nc.gpsimd, mksem("s_pool"), "pool")
        SP = Eng(nc.sync, mksem("s_sp"), "sp")
        for Q in (SP, POOL):
            Q.dsems = [mksem("%s_ds%d" % (Q.name, i)) for i in range(NDS)]
        ENGS = [PE, ACT, DVE, POOL, SP]

        def barrier():
            evs = [E.cur() for E in ENGS if E.n > 0]
            for Q in (SP, POOL):
                for i in range(min(Q.di, NDS)):
                    last = ((Q.di - 1 - i) // NDS) * NDS + i
                    cnt = (Q.di - 1 - i) // NDS + 1
                    evs.append((Q.dsems[i], 16 * cnt, "%s_d%d" % (Q.name, i)))
            for E in ENGS:
                for ev in evs:
                    if ev[2] == E.name:
                        continue
                    E.wait(ev)

        ps = es.enter_context(nc.psum_tensor("ps", [128, 8, 512], F32))
        PB = [Buf() for _ in range(8)]

        def bank(b, n=1):
            return ps[:, b:b + n, :].rearrange("p b c -> p (b c)")

        def bank16(b, n=1):
            return ps[:, b:b + n, :].rearrange("p b c -> p (b c)").bitcast(BF16)

        RING = 5
        wring = [sb("wr%d" % i, [128, 8, 512], BF16) for i in range(RING)]
        wringB = [Buf() for _ in range(RING)]
        wstate = {"i": 0}
        xT = sb("xT", [128, 16, 512], BF16)
        xTB = Buf()
        identF = sb("identF", [128, 128], F32)
        identB = sb("identB", [128, 128], BF16)
        onesF = sb("onesF", [128, 128], F32)
        colp = sb("colp", [128, 256], F32)
        qkn_sb = sb("qkn_sb", [128, 128], F32)
        esink = sb("esink", [128, 16], F32)
        gnw_sb = sb("gnw_sb", [128, 1024], F32)
        junk = sb("junk", [128, 1024], F32)
        junkB = Buf()
        rbc = sb("rbc", [128, 512], F32)
        rbcB = Buf()
        nst = sb("nst", [128, 16], F32)
        nstB = Buf()
        diag = [sb("diag%d" % i, [128, 128], F32) for i in range(2)]
        diagB = [Buf(), Buf()]
        tmpf = [sb("tmpf%d" % i, [128, 516], F32) for i in range(2)]
        tmpfB = [Buf(), Buf()]
        tmpg = [sb("tmpg%d" % i, [128, 512], F32) for i in range(2)]
        tmpgB = [Buf(), Buf()]
        tmph = [sb("tmph%d" % i, [128, 512], F32) for i in range(2)]
        tmphB = [Buf(), Buf()]
        sst = sb("sst", [128, 64], F32)
        sstB = Buf()
        CONST = Buf()
        tstate = {"i": 0}

        for idt in (identF, identB):
            op(POOL, lambda: nc.gpsimd.memset(idt[:], 0.0), (), (CONST,))
            op(POOL, lambda: nc.gpsimd.affine_select(out=idt[:], in_=idt[:], pattern=[[-1, 128]],
                                                     compare_op=ALU.not_equal, fill=1.0, base=0,
                                                     channel_multiplier=1), (CONST,), (CONST,))
        op(POOL, lambda: nc.gpsimd.memset(onesF[:], 1.0), (), (CONST,))

        WI = [[Buf() for _ in range(2)] for _ in range(9)]
        WO = [[Buf() for _ in range(2)] for _ in range(4)]
        WU = [[Buf() for _ in range(22)] for _ in range(2)]
        WD = [[Buf() for _ in range(6)] for _ in range(4)]
        for nb in range(9):
            cw = 512 if nb < 8 else 256
            for kt in range(2):
                src = w_in[kt * 1024:(kt + 1) * 1024, nb * 512:nb * 512 + cw].rearrange("(k p) c -> p k c", p=128)
                dma(POOL, wi[nb, kt, :, :, 0:cw], src, (), (WI[nb][kt],))
        prm_sb = sb("prm_sb", [128, 2, 128], F32)
        dma(SP, prm_sb[:, 0, :], prm[0:128, :], (), (CONST,))
        dma(SP, prm_sb[0:80, 1, :], prm[128:208, :], (), (CONST,))
        dma(SP, qkn_sb[:], qkn.partition_broadcast(128), (), (CONST,))
        dma(SP, esink[:], sinks.partition_broadcast(128), (), (CONST,))
        dma(SP, gnw_sb[:], gnw.partition_broadcast(128), (), (CONST,))
        use(PE, (CONST,), (PB[0],))
        nc.tensor.transpose(bank(0)[:, 0:128], prm_sb[:, 0, :], identF[:])
        ins = nc.tensor.transpose(bank(0)[:, 128:208], prm_sb[0:80, 1, :], identF[0:80, 0:80])
        fin(PE, ins, (CONST,), (PB[0],))
        op(ACT, lambda: nc.scalar.copy(out=colp[:, 0:208], in_=bank(0)[:, 0:208]), (PB[0],), (CONST,))
        op(ACT, lambda: nc.scalar.activation(out=esink[:], in_=esink[:], func=AF.Exp), (CONST,), (CONST,))

        def cwcol(j, kc):
            return colp[:, j * 44 + kc:j * 44 + kc + 1]

        def cbcol(kc):
            return colp[:, 132 + kc:133 + kc]

        for nb in range(4):
            for kt in range(2):
                src = w_o[kt * 1024:(kt + 1) * 1024, nb * 512:(nb + 1) * 512].rearrange("(k p) c -> p k c", p=128)
                dma(POOL, wo[nb, kt], src, (), (WO[nb][kt],))
        for j in range(22):
            for gu in range(2):
                src = w_up[:, gu * DFF + j * 256:gu * DFF + (j + 1) * 256].rearrange("(k p) c -> p k c", p=128)
                dma(POOL, wu[gu, j], src, (), (WU[gu][j],))
        for nb in range(4):
            for kt in range(6):
                nk = 8 if kt < 5 else 4
                src = w_down[kt * 1024:kt * 1024 + nk * 128, nb * 512:(nb + 1) * 512].rearrange("(k p) c -> p k c", p=128)
                dma(POOL, wd[nb, kt, :, 0:nk, :], src, (), (WD[nb][kt],))

        def wload(src_ap, srcB, nk=8, cw=512):
            i = wstate["i"]
            wstate["i"] += 1
            s = i % RING
            dma(SP, wring[s][:, 0:nk, 0:cw], src_ap, (srcB,), (wringB[s],))
            return wring[s], wringB[s]

        def rsqrt_ip(ap, B):
            op(ACT, lambda: nc.scalar.activation(out=ap, in_=ap, func=AF.Ln), (B,), (B,))
            op(ACT, lambda: nc.scalar.activation(out=ap, in_=ap, func=AF.Exp, scale=-0.5), (B,), (B,))

        def phase_norm(slots, wofs):
            toff = 0
            offs = []
            for c, (h, hB, nt) in enumerate(slots):
                offs.append(toff)
                toff += nt
            T = toff
            for c, (h, hB, nt) in enumerate(slots):
                for hf in range(2):
                    op(ACT, lambda: nc.scalar.activation(out=junk[:nt, :], in_=h[:nt, hf * 1024:(hf + 1) * 1024],
                                                         func=AF.Square, accum_out=nst[:nt, 2 * c + hf:2 * c + hf + 1]),
                       (hB,), (junkB, nstB))
                op(DVE, lambda: nc.vector.tensor_tensor(out=nst[:nt, 8 + c:9 + c], in0=nst[:nt, 2 * c:2 * c + 1],
                                                        in1=nst[:nt, 2 * c + 1:2 * c + 2], op=ALU.add),
                   (nstB,), (nstB,))
                op(DVE, lambda: nc.vector.tensor_scalar(out=nst[:nt, 8 + c:9 + c], in0=nst[:nt, 8 + c:9 + c],
                                                        scalar1=1.0 / D, scalar2=EPS, op0=ALU.mult, op1=ALU.add),
                   (nstB,), (nstB,))
                rsqrt_ip(nst[:nt, 8 + c:9 + c], nstB)
                dg, dgB = diag[c % 2], diagB[c % 2]
                op(DVE, lambda: nc.vector.tensor_scalar(out=dg[:nt, :nt], in0=identF[:nt, :nt],
                                                        scalar1=nst[:nt, 8 + c:9 + c], scalar2=None, op0=ALU.mult),
                   (nstB, CONST), (dgB,))
                op(PE, lambda: nc.tensor.matmul(bank(7)[:, offs[c]:offs[c] + nt], lhsT=onesF[:nt, :], rhs=dg[:nt, :nt],
                                                start=True, stop=True), (dgB, CONST), (PB[7],))
            op(ACT, lambda: nc.scalar.copy(out=rbc[:, 0:T], in_=bank(7)[:, 0:T]), (PB[7],), (rbcB,))
            for k in range(16):
                b = k % 4
                use(PE, [s[1] for s in slots] + [CONST], (PB[b],))
                for c, (h, hB, nt) in enumerate(slots):
                    ins = nc.tensor.transpose(bank(b)[:, offs[c]:offs[c] + nt], h[:nt, k * 128:(k + 1) * 128],
                                              identF[:nt, :nt])
                fin(PE, ins, [s[1] for s in slots], (PB[b],))
                op(DVE, lambda: nc.vector.scalar_tensor_tensor(out=xT[:, k, 0:T], in0=bank(b)[:, 0:T],
                                                               scalar=colp[:, wofs + k:wofs + k + 1], in1=rbc[:, 0:T],
                                                               op0=ALU.mult, op1=ALU.mult),
                   (PB[b], rbcB, CONST), (xTB,))
            return offs, T

        def proj_tm(slots, offs, lhs_fn, lhsB, nkc, tiles_fn, ncol, evac, active=None, set0=0, hook=None):
            for nbi, nb in enumerate(ncol):
                base = 4 * ((nbi + set0) % 2)
                act_slots = [c for c in range(len(slots)) if active is None or active(nb, c)]
                tiles = tiles_fn(nb)
                k0 = 0
                for ti, (src, srcB, nk, cw) in enumerate(tiles):
                    wt, wtB = wload(src, srcB, nk, cw)
                    wr = [PB[base + c] for c in act_slots] if ti == 0 else []
                    use(PE, (wtB, lhsB), wr)
                    ins = None
                    for c in act_slots:
                        nt = slots[c][2]
                        for kk in range(nk):
                            ins = nc.tensor.matmul(bank(base + c)[:nt, 0:cw], lhsT=lhs_fn(c, k0 + kk),
                                                   rhs=wt[:, kk, 0:cw], start=(k0 + kk == 0),
                                                   stop=(k0 + kk == nkc - 1))
                    fin(PE, ins, (wtB, lhsB), [PB[base + c] for c in act_slots])
                    k0 += nk
                if hook is not None and nbi == 0:
                    hook()
                for c in act_slots:
                    evac(nb, c, bank(base + c), PB[base + c])

        pstate = {"on": False}

        def tt2(out, in0, in1, op_, reads, writes):
            if pstate["on"]:
                op(POOL, lambda: nc.gpsimd.tensor_tensor(out=out, in0=in0, in1=in1, op=op_), reads, writes)
            else:
                op(DVE, lambda: nc.vector.tensor_tensor(out=out, in0=in0, in1=in1, op=op_), reads, writes)

        def rope_ops_ap(src, srcB, dstap, dstB, H, nt, rtab_, rtB, scaled):
            o = 64 if scaled else 0
            C = rtab_[:nt, o:o + 32].unsqueeze(1).to_broadcast([nt, H, 32])
            Sn = rtab_[:nt, o + 32:o + 64].unsqueeze(1).to_broadcast([nt, H, 32])
            x = src[:nt, 0:H * 64].rearrange("p (h t d) -> p h t d", h=H, t=2)
            y = dstap.rearrange("p h (t d) -> p h t d", t=2)
            i = tstate["i"] % 2
            tstate["i"] += 1
            tg, tgB = tmpg[i], tmpgB[i]
            t = tg[:nt, 0:H * 64].rearrange("p (h t d) -> p h t d", h=H, t=2)
            op(DVE, lambda: nc.vector.tensor_tensor(out=t[:, :, 0, :], in0=x[:, :, 0, :], in1=C, op=ALU.mult),
               (srcB, rtB), (tgB,))
            op(DVE, lambda: nc.vector.tensor_tensor(out=t[:, :, 1, :], in0=x[:, :, 1, :], in1=Sn, op=ALU.mult),
               (srcB, rtB), (tgB,))
            op(DVE, lambda: nc.vector.tensor_tensor(out=y[:, :, 0, :], in0=t[:, :, 0, :], in1=t[:, :, 1, :],
                                                    op=ALU.subtract), (tgB,), (dstB,))
            th, thB = tmph[i], tmphB[i]
            u = th[:nt, 0:H * 64].rearrange("p (h t d) -> p h t d", h=H, t=2)
            tt2(u[:, :, 0, :], x[:, :, 1, :], C, ALU.mult, (srcB, rtB), (thB,))
            tt2(u[:, :, 1, :], x[:, :, 0, :], Sn, ALU.mult, (srcB, rtB), (thB,))
            tt2(y[:, :, 1, :], u[:, :, 0, :], u[:, :, 1, :], ALU.add, (thB,), (dstB,))

        def rope_ops(src, srcB, dst, dstB, H, nt, rtab_, rtB, scaled):
            rope_ops_ap(src, srcB, dst[:nt, 0:H * 64].rearrange("p (h d) -> p h d", h=H), dstB, H, nt, rtab_, rtB,
                        scaled)

        def headnorm(tf, tfB, H, nt, wcol0):
            op(ACT, lambda: nc.scalar.activation(out=junk[:nt, 0:H * 64], in_=tf[:nt, 0:H * 64], func=AF.Square),
               (tfB,), (junkB,))
            op(DVE, lambda: nc.vector.tensor_reduce(out=sst[:nt, 0:H],
                                                    in_=junk[:nt, 0:H * 64].rearrange("p (h d) -> p h d", h=H),
                                                    axis=AX.X, op=ALU.add), (junkB,), (sstB,))
            op(DVE, lambda: nc.vector.tensor_scalar(out=sst[:nt, 0:H], in0=sst[:nt, 0:H], scalar1=1.0 / 64,
                                                    scalar2=EPS, op0=ALU.mult, op1=ALU.add), (sstB,), (sstB,))
            rsqrt_ip(sst[:nt, 0:H], sstB)
            v = tf[:nt, 0:H * 64].rearrange("p (h d) -> p h d", h=H)
            op(DVE, lambda: nc.vector.tensor_tensor(out=v, in0=v, in1=sst[:nt, 0:H].unsqueeze(2).to_broadcast([nt, H, 64]),
                                                    op=ALU.mult), (tfB, sstB), (tfB,))
            op(DVE, lambda: nc.vector.tensor_tensor(out=v, in0=v,
                                                    in1=qkn_sb[:nt, wcol0:wcol0 + 64].unsqueeze(1).to_broadcast([nt, H, 64]),
                                                    op=ALU.mult), (tfB, CONST), (tfB,))

        with contextlib.ExitStack() as es2:
            def sb2(name, shape, dt=F32):
                return es2.enter_context(nc.sbuf_tensor(name, list(shape), dt))

            hs = [sb2("h%d" % i, [128, D], F32) for i in range(4)]
            hB = [Buf() for _ in range(4)]
            NS = 4
            big = sb2("big", [128, NKC * 512], BF16)
            SLOT = 4608
            rqb = [big[:, i * SLOT + 0:i * SLOT + 512] for i in range(NS)]
            kbb = [big[:, i * SLOT + 512:i * SLOT + 1024] for i in range(NS)]
            kdec = [big[:, i * SLOT + 1024:i * SLOT + 1536] for i in range(NS)]
            rvb = [big[:, i * SLOT + 1536:i * SLOT + 2560] for i in range(NS)]
            rgs = [big[:, i * SLOT + 2560:i * SLOT + 3584] for i in range(NS)]
            sqb = [big[:, i * SLOT + 3584:i * SLOT + 4608] for i in range(NS)]
            skf = [sb2("skf%d" % i, [128, 128], F32) for i in range(NS)]
            svf = [sb2("svf%d" % i, [128, 128], F32) for i in range(NS)]
            skb = [sb2("skb%d" % i, [128, 128], BF16) for i in range(NS)]
            slotB = [Buf() for _ in range(NS)]
            act = big[:, :].rearrange("p (k t) -> p k t", k=NKC)
            actB = Buf()
            guard = {"pe": None}
            rtab = [sb2("rtab%d" % i, [128, 128], F32) for i in range(NS)]
            rtabB = [Buf() for _ in range(NS)]
            dmask = sb2("dmask", [128, 1024], F32)
            gq = sb2("gq", [128, 512], F32)
            kd = sb2("kd", [128, 8], F32)
            gS = sb2("gS", [128, 4], F32)
            mcur = sb2("mcur", [128, 128], F32)
            mprev = sb2("mprev", [128, 128], F32)
            mprev1 = sb2("mprev1", [128, 128], F32)
            for t_, d_ in ((dmask, c_dmask), (gq, c_gq), (kd, c_kd), (gS, c_gS), (mcur, c_mcur), (mprev, c_mprev),
                           (mprev1, c_mprev1)):
                dma(SP, t_[:], d_, (), (CONST,))
            S = sb2("S", [128, 4, 128], F32)
            SB_ = Buf()
            Sb = [sb2("Sb%d" % i, [128, 4, 128], BF16) for i in range(2)]
            SbB = [Buf(), Buf()]
            skT = [sb2("skT%d" % i, [128, 128], BF16) for i in range(2)]
            svx = [sb2("svx%d" % i, [128, 2, 65], BF16) for i in range(2)]
            kvB = [Buf(), Buf()]
            qkT = sb2("qkT", [128, 8, 128], BF16)
            qkTB = Buf()
            qgT = sb2("qgT", [128, 4, 128], BF16)
            qgTB = Buf()
            sqT = sb2("sqT", [128, 8, 128], BF16)
            sqTB = Buf()
            scm = sb2("scm", [128, 1024], BF16)
            scmB = Buf()
            osb = sb2("osb", [128, 1024], F32)
            osbB = Buf()
            gst = sb2("gst", [128, 64], F32)
            gstB = Buf()
            pT = [sb2("pT%d" % i, [128, 1024], BF16) for i in range(2)]
            pTB = [Buf(), Buf()]
            cat0 = sb2("cat0", [128, D], BF16)
            cat = [cat0, cat0]
            catB0 = Buf()
            catB = [catB0, catB0]
            gcar = sb2("gcar", [128, 2, NKC], F32)
            gcarB = Buf()
            gsb, gsbB = tmpf, tmpfB
            cva, cvaB = tmpg, tmpgB

            op(DVE, lambda: nc.vector.memset(S[:], 0.0), (), (SB_,))
            op(DVE, lambda: nc.vector.memset(Sb[0][:], 0.0), (), (SbB[0],))
            op(DVE, lambda: nc.vector.memset(gcar[:], 0.0), (), (gcarB,))
            for i in range(2):
                op(DVE, lambda: nc.vector.memset(svx[i][:], 1.0), (), (kvB[i],))
                op(DVE, lambda: nc.vector.memset(skT[i][:], 0.0), (), (kvB[i],))
            st = {"sp": 0, "kv": 0, "cat": 0}

            def tiles_wi(nb):
                cw = 512 if nb < 8 else 256
                return [(wi[nb, kt, :, :, 0:cw], WI[nb][kt], 8, cw) for kt in range(2)]

            def tiles_wo(nb):
                return [(wo[nb, kt], WO[nb][kt], 8, 512) for kt in range(2)]

            def tiles_wd(nb):
                return [(wd[nb, kt, :, 0:(8 if kt < 5 else 4), :], WD[nb][kt], 8 if kt < 5 else 4, 512)
                        for kt in range(6)]

            def evac_in(slots, kinds):
                def ev(nb, c, pa, pB):
                    nt = slots[c][2]
                    sB = slotB[c]
                    if nb in (0, 1):
                        i = tstate["i"] % 2
                        tf, tfB = tmpf[i], tmpfB[i]
                        op(ACT, lambda: nc.scalar.copy(out=tf[:nt, 0:512], in_=pa[:nt, :]), (pB,), (tfB,))
                        if nb == 0:
                            rope_ops(tf, tfB, rqb[c], sB, 8, nt, rtab[c], rtabB[c], False)
                        else:
                            rope_ops(tf, tfB, kbb[c], sB, 8, nt, rtab[c], rtabB[c], True)
                            tt2(kdec[c][:nt, :].rearrange("p (h d) -> p h d", h=8),
                                kbb[c][:nt, :].rearrange("p (h d) -> p h d", h=8),
                                kd[:nt, :].unsqueeze(2).to_broadcast([nt, 8, 64]), ALU.mult, (sB, CONST), (sB,))
                    elif nb in (2, 3):
                        op(ACT, lambda: nc.scalar.copy(out=rvb[c][:nt, (nb - 2) * 512:(nb - 1) * 512], in_=pa[:nt, :]),
                           (pB,), (sB,))
                    elif nb in (4, 5):
                        op(ACT, lambda: nc.scalar.activation(out=rgs[c][:nt, (nb - 4) * 512:(nb - 3) * 512],
                                                             in_=pa[:nt, :], func=AF.Silu), (pB,), (sB,))
                    elif nb in (6, 7):
                        i = tstate["i"] % 2
                        tf, tfB = tmpf[i], tmpfB[i]
                        op(ACT, lambda: nc.scalar.copy(out=tf[:nt, 0:512], in_=pa[:nt, :]), (pB,), (tfB,))
                        headnorm(tf, tfB, 8, nt, 0)
                        rope_ops_ap(tf, tfB,
                                    sqb[c][:nt, :].rearrange("p (g k d) -> p g k d", g=8, k=2)[:, :, nb - 6, :],
                                    sB, 8, nt, rtab[c], rtabB[c], False)
                    else:
                        i = tstate["i"] % 2
                        tf, tfB = tmpf[i], tmpfB[i]
                        op(ACT, lambda: nc.scalar.copy(out=tf[:nt, 0:256], in_=pa[:nt, 0:256]), (pB,), (tfB,))
                        headnorm(tf, tfB, 2, nt, 64)
                        rope_ops(tf, tfB, skf[c], sB, 2, nt, rtab[c], rtabB[c], False)
                        op(ACT, lambda: nc.scalar.copy(out=skb[c][:nt, :], in_=skf[c][:nt, :]), (sB,), (sB,))
                        op(ACT, lambda: nc.scalar.copy(out=svf[c][:nt, :], in_=tf[:nt, 128:256]), (tfB,), (sB,))
                return ev

            def state_update(c):
                use(PE, (slotB[c],), (PB[6], PB[7]))
                for h in range(8):
                    ins = nc.tensor.matmul(bank(6 + h // 4)[:, (h % 4) * 128:(h % 4 + 1) * 128],
                                           lhsT=kdec[c][:, (h // 2) * 128:(h // 2 + 1) * 128],
                                           rhs=rvb[c][:, h * 128:(h + 1) * 128], start=True, stop=True)
                fin(PE, ins, (slotB[c],), (PB[6], PB[7]))
                tt2(S[:], S[:], gS[:, :].unsqueeze(2).to_broadcast([128, 4, 128]), ALU.mult, (SB_, CONST), (SB_,))
                for bk in range(2):
                    pv = bank(6 + bk).rearrange("p (b t v) -> p b t v", b=2, t=2)
                    op(DVE, lambda: nc.vector.tensor_tensor(out=S[0:64, 2 * bk:2 * bk + 2, :],
                                                            in0=S[0:64, 2 * bk:2 * bk + 2, :], in1=pv[0:64, :, 0, :],
                                                            op=ALU.add), (SB_, PB[6 + bk]), (SB_,))
                    op(DVE, lambda: nc.vector.tensor_tensor(out=S[64:128, 2 * bk:2 * bk + 2, :],
                                                            in0=S[64:128, 2 * bk:2 * bk + 2, :], in1=pv[64:128, :, 1, :],
                                                            op=ALU.add), (SB_, PB[6 + bk]), (SB_,))
                st["sp"] ^= 1
                p = st["sp"]
                op(ACT, lambda: nc.scalar.copy(out=Sb[p][:], in_=S[:]), (SB_,), (SbB[p],))

            def kv_carry(c):
                st["kv"] ^= 1
                p = st["kv"]
                op(PE, lambda: nc.tensor.transpose(bank16(7)[:, 0:128], skb[c][:, :], identB[:]),
                   (slotB[c], CONST), (PB[7],))
                op(ACT, lambda: nc.scalar.copy(out=skT[p][:], in_=bank16(7)[:, 0:128]), (PB[7],), (kvB[p],))
                op(ACT, lambda: nc.scalar.copy(out=svx[p][:, :, 0:64],
                                               in_=svf[c][:, :].rearrange("p (k d) -> p k d", k=2)),
                   (slotB[c],), (kvB[p],))

            def gn_gate(c, nt, rg_t, cat_t, catBuf, ob_lo=4):
                op(ACT, lambda: nc.scalar.copy(out=osb[:nt, :], in_=bank(ob_lo, 2)[:nt, :]),
                   (PB[ob_lo], PB[ob_lo + 1]), (osbB,))
                o3 = osb[:nt, :].rearrange("p (h v) -> p h v", h=8)
                op(DVE, lambda: nc.vector.tensor_reduce(out=gst[:nt, 0:8], in_=o3, axis=AX.X, op=ALU.add),
                   (osbB,), (gstB,))
                op(ACT, lambda: nc.scalar.activation(out=junk[:nt, 0:1024], in_=osb[:nt, :], func=AF.Square),
                   (osbB,), (junkB,))
                op(DVE, lambda: nc.vector.tensor_reduce(out=gst[:nt, 8:16],
                                                        in_=junk[:nt, 0:1024].rearrange("p (h v) -> p h v", h=8),
                                                        axis=AX.X, op=ALU.add), (junkB,), (gstB,))
                op(DVE, lambda: nc.vector.tensor_scalar(out=gst[:nt, 0:8], in0=gst[:nt, 0:8], scalar1=1.0 / 128,
                                                        scalar2=None, op0=ALU.mult), (gstB,), (gstB,))
                op(DVE, lambda: nc.vector.tensor_tensor(out=gst[:nt, 16:24], in0=gst[:nt, 0:8], in1=gst[:nt, 0:8],
                                                        op=ALU.mult), (gstB,), (gstB,))
                op(DVE, lambda: nc.vector.scalar_tensor_tensor(out=gst[:nt, 8:16], in0=gst[:nt, 8:16],
                                                               scalar=1.0 / 128, in1=gst[:nt, 16:24],
                                                               op0=ALU.mult, op1=ALU.subtract), (gstB,), (gstB,))
                op(DVE, lambda: nc.vector.tensor_scalar(out=gst[:nt, 8:16], in0=gst[:nt, 8:16], scalar1=EPS,
                                                        scalar2=None, op0=ALU.add), (gstB,), (gstB,))
                rsqrt_ip(gst[:nt, 8:16], gstB)
                op(DVE, lambda: nc.vector.tensor_tensor(out=o3, in0=o3,
                                                        in1=gst[:nt, 0:8].unsqueeze(2).to_broadcast([nt, 8, 128]),
                                                        op=ALU.subtract), (osbB, gstB), (osbB,))
                op(DVE, lambda: nc.vector.tensor_tensor(out=o3, in0=o3,
                                                        in1=gst[:nt, 8:16].unsqueeze(2).to_broadcast([nt, 8, 128]),
                                                        op=ALU.mult), (osbB, gstB), (osbB,))
                op(DVE, lambda: nc.vector.tensor_tensor(out=osb[:nt, :], in0=osb[:nt, :], in1=gnw_sb[:nt, :],
                                                        op=ALU.mult), (osbB, CONST), (osbB,))
                op(DVE, lambda: nc.vector.tensor_tensor(out=cat_t[:nt, 0:1024], in0=osb[:nt, :], in1=rg_t[:nt, :],
                                                        op=ALU.mult), (osbB, slotB[c]), (catBuf,))

            def swa_norm(nt, cat_t, catBuf, ob=4):
                for seg, (h0, n) in enumerate(((0, 7), (7, 7), (14, 2))):
                    v = bank(ob + seg)[:nt, 0:n * 65].rearrange("p (h e) -> p h e", e=65)
                    op(DVE, lambda: nc.vector.tensor_tensor(out=gst[:nt, 32 + h0:32 + h0 + n], in0=v[:, :, 64],
                                                            in1=esink[:nt, h0:h0 + n], op=ALU.add),
                       (PB[ob + seg], CONST), (gstB,))
                    op(DVE, lambda: nc.vector.reciprocal(out=gst[:nt, 32 + h0:32 + h0 + n],
                                                         in_=gst[:nt, 32 + h0:32 + h0 + n]), (gstB,), (gstB,))
                    op(DVE, lambda: nc.vector.tensor_tensor(
                        out=cat_t[:nt, 1024 + h0 * 64:1024 + (h0 + n) * 64].rearrange("p (h d) -> p h d", d=64),
                        in0=v[:, :, 0:64],
                        in1=gst[:nt, 32 + h0:32 + h0 + n].unsqueeze(2).to_broadcast([nt, n, 64]), op=ALU.mult),
                       (PB[ob + seg], gstB), (catBuf,))

            def cat_to_xT(nt, cat_t, catBuf, off):
                for half in range(2):
                    b = half
                    use(PE, (catBuf, CONST), (PB[b],))
                    for k in range(8):
                        ins = nc.tensor.transpose(bank16(b)[:, k * 128:k * 128 + nt],
                                                  cat_t[:nt, (half * 8 + k) * 128:(half * 8 + k + 1) * 128],
                                                  identB[:nt, :nt])
                    fin(PE, ins, (catBuf,), (PB[b],))
                    op(ACT, lambda: nc.scalar.copy(
                        out=xT[:, half * 8:half * 8 + 8, off:off + nt],
                        in_=bank16(b)[:, 0:1024].rearrange("p (k t) -> p k t", k=8)[:, :, 0:nt]),
                       (PB[b],), (xTB,))

            def mixer_full(c, off, first_prev_mask):
                import os as _os
                KMIX = int(_os.environ.get("KMIX", "99"))
                sB = slotB[c]
                use(PE, (sB, CONST), (PB[0],))
                for b in range(4):
                    nc.tensor.transpose(bank16(0)[:, b * 128:(b + 1) * 128], rqb[c][:, b * 128:(b + 1) * 128], identB[:])
                for b in range(4):
                    ins = nc.tensor.transpose(bank16(0)[:, (4 + b) * 128:(5 + b) * 128],
                                              kbb[c][:, b * 128:(b + 1) * 128], identB[:])
                fin(PE, ins, (sB,), (PB[0],))
                op(ACT, lambda: nc.scalar.copy(out=qkT[:].rearrange("p b t -> p (b t)"), in_=bank16(0)[:, 0:1024]),
                   (PB[0],), (qkTB,))
                op(DVE, lambda: nc.vector.tensor_tensor(out=qgT[:].rearrange("p b t -> p (b t)"),
                                                        in0=qkT[:, 0:4, :].rearrange("p b t -> p (b t)"),
                                                        in1=gq[:, :], op=ALU.mult),
                   (qkTB, CONST), (qgTB,))
                use(PE, (sB, CONST), (PB[1],))
                for g in range(8):
                    ins = nc.tensor.transpose(bank16(1)[:, g * 128:(g + 1) * 128], sqb[c][:, g * 128:(g + 1) * 128],
                                              identB[:])
                fin(PE, ins, (sB,), (PB[1],))
                op(ACT, lambda: nc.scalar.copy(out=sqT[:].rearrange("p b t -> p (b t)"), in_=bank16(1)[:, 0:1024]),
                   (PB[1],), (sqTB,))
                if KMIX <= 1:
                    return
                use(PE, (qkTB,), (PB[2], PB[3]))
                for h in range(8):
                    hp = (h % 2) * 64
                    ins = nc.tensor.matmul(bank(2 + h % 2)[:, (h // 2) * 128:(h // 2 + 1) * 128],
                                           lhsT=qkT[hp:hp + 64, 4 + h // 2, :], rhs=qkT[hp:hp + 64, h // 2, :],
                                           start=True, stop=True)
                fin(PE, ins, (qkTB,), (PB[2], PB[3]))
                for i in range(2):
                    op(DVE, lambda: nc.vector.tensor_tensor(out=scm[:, i * 512:(i + 1) * 512], in0=bank(2 + i),
                                                            in1=dmask[:, i * 512:(i + 1) * 512], op=ALU.mult),
                       (PB[2 + i], CONST), (scmB,))
                if KMIX <= 2:
                    return
                p = st["sp"]
                use(PE, (scmB, sB, qgTB, SbB[p]), (PB[4], PB[5]))
                for h in range(8):
                    hp = (h % 2) * 64
                    o_ap = bank(4 + h // 4)[:, (h % 4) * 128:(h % 4 + 1) * 128]
                    hb = (h % 2) * 4 + h // 2
                    nc.tensor.matmul(o_ap, lhsT=scm[:, hb * 128:(hb + 1) * 128], rhs=rvb[c][:, h * 128:(h + 1) * 128],
                                     start=True, stop=False)
                    ins = nc.tensor.matmul(o_ap, lhsT=qgT[hp:hp + 64, h // 2, :], rhs=Sb[p][hp:hp + 64, h // 2, :],
                                           start=False, stop=True)
                fin(PE, ins, (scmB, sB, qgTB, SbB[p]), (PB[4], PB[5]))
                if KMIX <= 3:
                    return
                state_update(c)
                if KMIX <= 4:
                    return
                st["cat"] ^= 1
                ct, ctB = cat[st["cat"]], catB[st["cat"]]
                gn_gate(c, 128, rgs[c], ct, ctB)
                if KMIX <= 5:
                    return
                pprev = st["kv"]
                kv_carry(c)
                if KMIX <= 6:
                    return
                pcur = st["kv"]
                for kvh in range(2):
                    kp0 = kvh * 64
                    pars = (pprev, pcur)
                    for blk in range(2):
                        par = pars[blk]
                        mk = (first_prev_mask if first_prev_mask is not None else mprev) if blk == 0 else mcur
                        bb = 2 * blk
                        use(PE, (kvB[par], sqTB), (PB[bb], PB[bb + 1]))
                        nc.tensor.matmul(bank(bb), lhsT=skT[par][kp0:kp0 + 64, :],
                                         rhs=sqT[kp0:kp0 + 64, 0:4, :], start=True, stop=True)
                        ins = nc.tensor.matmul(bank(bb + 1), lhsT=skT[par][kp0:kp0 + 64, :],
                                               rhs=sqT[kp0:kp0 + 64, 4:8, :], start=True, stop=True)
                        fin(PE, ins, (kvB[par], sqTB), (PB[bb], PB[bb + 1]))
                        pt, ptB = pT[blk], pTB[blk]
                        op(ACT, lambda: nc.scalar.activation(out=pt[:, :], in_=bank(bb, 2), func=AF.Exp, scale=0.125),
                           (PB[bb], PB[bb + 1]), (ptB,))
                        tt2(pt[:, :].rearrange("p (g t) -> p g t", g=8), pt[:, :].rearrange("p (g t) -> p g t", g=8),
                            mk[:, :].unsqueeze(1).to_broadcast([128, 8, 128]), ALU.mult, (ptB, CONST), (ptB,))
                    wr = (PB[4], PB[5], PB[6]) if kvh == 0 else ()
                    rd = (pTB[0], pTB[1], kvB[pprev], kvB[pcur])
                    use(PE, rd, wr)
                    for g in range(8):
                        hq = kvh * 8 + g
                        o_ap = bank(4 + hq // 7)[:, (hq % 7) * 65:(hq % 7) * 65 + 65]
                        nc.tensor.matmul(o_ap, lhsT=pT[0][:, g * 128:(g + 1) * 128], rhs=svx[pprev][:, kvh, :],
                                         start=True, stop=False)
                        ins = nc.tensor.matmul(o_ap, lhsT=pT[1][:, g * 128:(g + 1) * 128], rhs=svx[pcur][:, kvh, :],
                                               start=False, stop=True)
                    fin(PE, ins, rd, (PB[4], PB[5], PB[6]))
                if KMIX <= 7:
                    return
                swa_norm(128, ct, ctB)
                if KMIX <= 8:
                    return
                cat_to_xT(128, ct, ctB, off)

            def phase_up(T, segs, g_only=False):
                if g_only:
                    for j in range(22):
                        tl = [wload(wu[0, j, :, kt * 8:(kt + 1) * 8, :], WU[0][j], 8, 256) for kt in range(2)]
                        banks = [2 * (j % 4), 2 * (j % 4) + 1]
                        for kt in range(2):
                            wt, wtB = tl[kt]
                            wr = [PB[banks[0]], PB[banks[1]]] if kt == 0 else []
                            use(PE, (wtB, xTB), wr)
                            for f in range(2):
                                for kk in range(8):
                                    k = kt * 8 + kk
                                    ins = nc.tensor.matmul(bank(banks[f])[:, 0:T], lhsT=wt[:, kk, f * 128:(f + 1) * 128],
                                                           rhs=xT[:, k, 0:T], start=(k == 0), stop=(k == 15))
                            fin(PE, ins, (wtB, xTB), [PB[banks[0]], PB[banks[1]]])
                        for f in range(2):
                            kc = 2 * j + f
                            op(DVE, lambda: nc.vector.tensor_copy(out=gcar[:, :, kc], in_=bank(banks[f])[:, T - 2:T]),
                               (PB[banks[f]],), (gcarB,))
                    return
                for j in range(22):
                    tl = []
                    for gu in range(2):
                        for kt in range(2):
                            tl.append(wload(wu[gu, j, :, kt * 8:(kt + 1) * 8, :], WU[gu][j], 8, 256))
                    s = j % 2
                    banks = [4 * s, 4 * s + 1, 4 * s + 2, 4 * s + 3]
                    for gu in range(2):
                        for kt in range(2):
                            wt, wtB = tl[gu * 2 + kt]
                            wr = [PB[banks[gu * 2]], PB[banks[gu * 2 + 1]]] if kt == 0 else []
                            use(PE, (wtB, xTB), wr)
                            for f in range(2):
                                for kk in range(8):
                                    k = kt * 8 + kk
                                    ins = nc.tensor.matmul(bank(banks[gu * 2 + f])[:, 0:T],
                                                           lhsT=wt[:, kk, f * 128:(f + 1) * 128], rhs=xT[:, k, 0:T],
                                                           start=(k == 0), stop=(k == 15))
                            fin(PE, ins, (wtB, xTB), [PB[banks[gu * 2]], PB[banks[gu * 2 + 1]]])
                    for f in range(2):
                        kc = 2 * j + f
                        bg, bu = banks[f], banks[2 + f]
                        gi = kc % 2
                        g_, gB = gsb[gi], gsbB[gi]
                        a_, aB = cva[gi], cvaB[gi]
                        op(ACT, lambda: nc.scalar.copy(out=g_[:, 2:2 + T], in_=bank(bg)[:, 0:T]), (PB[bg],), (gB,))
                        op(ACT, lambda: nc.scalar.activation(out=a_[:, 0:T], in_=bank(bg)[:, 0:T], func=AF.Identity,
                                                             scale=cwcol(2, kc), bias=cbcol(kc)),
                           (PB[bg], CONST), (aB,))
                        for (off, n) in segs:
                            op(DVE, lambda: nc.vector.tensor_copy(out=g_[:, off:off + 2], in_=gcar[:, :, kc]),
                               (gcarB,), (gB,))
                            op(DVE, lambda: nc.vector.tensor_copy(out=gcar[:, :, kc], in_=g_[:, off + n:off + n + 2]),
                               (gB,), (gcarB,))
                            op(DVE, lambda: nc.vector.scalar_tensor_tensor(out=a_[:, off:off + n], in0=g_[:, off + 1:off + 1 + n],
                                                                           scalar=cwcol(1, kc), in1=a_[:, off:off + n],
                                                                           op0=ALU.mult, op1=ALU.add),
                               (gB, aB, CONST), (aB,))
                            op(DVE, lambda: nc.vector.scalar_tensor_tensor(out=a_[:, off:off + n], in0=g_[:, off:off + n],
                                                                           scalar=cwcol(0, kc), in1=a_[:, off:off + n],
                                                                           op0=ALU.mult, op1=ALU.add),
                               (gB, aB, CONST), (aB,))
                        op(ACT, lambda: nc.scalar.activation(out=a_[:, 0:T], in_=a_[:, 0:T], func=AF.Silu), (aB,), (aB,))
                        op(DVE, lambda: nc.vector.tensor_tensor(out=act[:, kc, 0:T], in0=a_[:, 0:T], in1=bank(bu)[:, 0:T],
                                                                op=ALU.mult), (aB, PB[bu]), (actB,))

            preloaded = set()

            def preload(next_cis, c):
                if next_cis is None or c >= len(next_cis) or next_cis[c] in preloaded:
                    return
                preloaded.add(next_cis[c])
                ci_ = next_cis[c]
                dma(SP, hs[c][:, :], xc[ci_ * 128:(ci_ + 1) * 128, :], (), (hB[c],))

            def run_group(cis, light, tail_only=False, next_cis=None):
                slots = [(hs[c], hB[c], 128) for c in range(len(cis))]
                for c, ci in enumerate(cis):
                    if ci not in preloaded:
                        dma(SP, hs[c][:, :], xc[ci * 128:(ci + 1) * 128, :], (), (hB[c],))
                    dma(SP, rtab[c][:, :], rope[ci], (), (rtabB[c],))
                for c in range(len(cis), 4):
                    preload(next_cis, c)
                offs, T = phase_norm(slots, 176)

                def light_hook():
                    for c in range(len(cis)):
                        preload(next_cis, c)
                if guard["pe"] is not None:
                    ACT.wait(guard["pe"])
                    DVE.wait(guard["pe"])
                ev = evac_in(slots, None)
                if light:
                    has14 = 14 in cis
                    cols = [1, 2, 3] + ([8] if has14 else [])
                    proj_tm(slots, offs, lambda c, k: xT[:, k, offs[c]:offs[c] + 128], xTB, 16, tiles_wi, cols, ev,
                            active=lambda nb, c: (nb != 8) or cis[c] == 14, hook=light_hook)
                    for c, ci in enumerate(cis):
                        state_update(c)
                        if ci == 14:
                            kv_carry(c)
                    return
                import os as _os
                KSUB = int(_os.environ.get("KSUB", "99"))
                KCOLS = int(_os.environ.get("KCOLS", "9"))
                proj_tm(slots, offs, lambda c, k: xT[:, k, offs[c]:offs[c] + 128], xTB, 16, tiles_wi,
                        [0, 2, 1, 3, 6, 4, 7, 5, 8][:KCOLS], ev)
                if KSUB <= 1:
                    return
                KMIXC = int(_os.environ.get("KMIXC", "99"))
                for c, ci in enumerate(cis):
                    if int(_os.environ.get("KMIXS", "0")) <= c < KMIXC:
                        mixer_full(c, offs[c], mprev1 if ci == 16 else None)
                if KSUB <= 2:
                    return

                def ev_o(nb, c, pa, pB):
                    op(DVE, lambda: nc.vector.tensor_tensor(out=hs[c][:, nb * 512:(nb + 1) * 512], in0=pa[:, :],
                                                            in1=hs[c][:, nb * 512:(nb + 1) * 512], op=ALU.add),
                       (pB, hB[c]), (hB[c],))
                proj_tm(slots, offs, lambda c, k: xT[:, k, offs[c]:offs[c] + 128], xTB, 16, tiles_wo, list(range(4)),
                        ev_o, set0=1)
                if KSUB <= 3:
                    return
                offs, T = phase_norm(slots, 192)
                for E_ in (PE, ACT, DVE):
                    DVE.wait(E_.cur())
                if tail_only:
                    for c in range(len(cis)):
                        preload(next_cis, c)
                phase_up(T, [(0, T)], g_only=tail_only)
                if KSUB <= 5 or tail_only:
                    return
                if 31 in cis:
                    c = cis.index(31)
                    op(PE, lambda: nc.tensor.transpose(bank(0)[0:88, 0:128],
                                                       gcar[:, :, :].rearrange("p t k -> p (t k)"), identF[:]),
                       (gcarB, CONST), (PB[0],))
                    op(ACT, lambda: nc.scalar.copy(out=junk[0:88, 0:128], in_=bank(0)[0:88, 0:128]), (PB[0],), (junkB,))
                    for t in range(2):
                        dma(SP, cp[t].rearrange("(k p) -> k p", p=128), junk[t * 44:(t + 1) * 44, 0:128], (junkB,), ())
                    dma(SP, kp[:, :], skf[c][:, :], (slotB[c],), ())
                    dma(SP, vp[:, :], svf[c][:, :], (slotB[c],), ())
                    for hp in range(2):
                        dma(SP, rsp.rearrange("(b hp) d v -> hp d b v", hp=2)[hp], S[hp * 64:(hp + 1) * 64, :, :],
                            (SB_,), ())

                def ev_d(nb, c, pa, pB):
                    op(DVE, lambda: nc.vector.tensor_tensor(out=hs[c][:, nb * 512:(nb + 1) * 512], in0=pa[:, :],
                                                            in1=hs[c][:, nb * 512:(nb + 1) * 512], op=ALU.add),
                       (pB, hB[c]), (hB[c],))
                    if nb == 3:
                        ci = cis[c]
                        dma(SP, yp[(ci - 16) * 128:(ci - 15) * 128, :], hs[c][:, :], (hB[c],), ())
                        preload(next_cis, c)
                proj_tm(slots, offs, lambda c, k: act[:, k, offs[c]:offs[c] + 128], actB, NKC, tiles_wd,
                        list(range(4)), ev_d, active=lambda nb, c: cis[c] >= 16)
                guard["pe"] = PE.cur()

            import os as _os
            KSTOP = int(_os.environ.get("KSTOP", "99"))
            glist = [([ci for ci in range(g0, min(g0 + 4, 15))], True) for g0 in (0, 4, 8, 12)]
            glist += [([15], False)]
            glist += [(list(range(g0, g0 + 4)), False) for g0 in (16, 20, 24, 28)]
            for gi, (cis_, light_) in enumerate(glist):
                if gi >= KSTOP:
                    break
                pstate["on"] = (not light_) and cis_[0] >= 20
                run_group(cis_, light_, tail_only=(cis_ == [15]),
                          next_cis=(glist[gi + 1][0] if gi + 1 < len(glist) else None))
            barrier()

        import os as _os2
        if ENABLE_SAMPLE and _os2.environ.get("KSAMPLE", "1") == "1":
            with contextlib.ExitStack() as es3:
                def sb3(name, shape, dt=F32):
                    return es3.enter_context(nc.sbuf_tensor(name, list(shape), dt))

                NT = 64
                pstate["on"] = True
                hS = sb3("hS", [128, D], F32)
                hSB = Buf()
                rqb = [sb3("rqb_s", [128, 512], BF16)]
                kbb = [sb3("kbb_s", [128, 512], BF16)]
                kdec = [sb3("kdec_s", [128, 512], BF16)]
                rvb = [sb3("rvb_s", [128, 1024], BF16)]
                rgs = [sb3("rgs_s", [128, 1024], BF16)]
                sqb = [sb3("sqb_s", [128, 1024], BF16)]
                skf = [sb3("skf_s", [128, 128], F32)]
                svf = [sb3("svf_s", [128, 128], F32)]
                skb = [sb3("skb_s", [128, 128], BF16)]
                slotB = [Buf()]
                rtab = [sb3("rtab_s", [128, 128], F32)]
                rtabB = [Buf()]
                osb = sb3("osb_s", [128, 1024], F32)
                osbB = Buf()
                gst = sb3("gst_s", [128, 64], F32)
                gstB = Buf()
                kd = sb3("kd_s", [128, 8], F32)
                t_dmask = sb3("t_dmask", [128, 512], F32)
                t_gq = sb3("t_gq", [128, 512], F32)
                t_g4 = sb3("t_g4", [128, 8], F32)
                t_sel2 = sb3("t_sel2", [128, 512], F32)
                t_selk = sb3("t_selk", [128, 16], F32)
                t_mc = sb3("t_mc", [128, 1024], F32)
                t_mn = sb3("t_mn", [128, 64], F32)
                dma(SP, kd[0:64, :], s_kd, (), (CONST,))
                dma(SP, t_dmask[0:64, :], s_dmask, (), (CONST,))
                dma(SP, t_gq[:, :], s_gq, (), (CONST,))
                dma(SP, t_g4[:, :], s_g4, (), (CONST,))
                dma(SP, t_sel2[:, :], s_sel2, (), (CONST,))
                dma(SP, t_selk[0:64, :], s_selk, (), (CONST,))
                dma(SP, t_mc[:, :], s_mc, (), (CONST,))
                dma(SP, t_mn[0:64, :], s_mn, (), (CONST,))
                cat_s = sb3("cat_s", [128, D], BF16)
                catSB = Buf()
                act_s = sb3("act_s", [128, NKC, 64], BF16)
                actSB = Buf()
                kcT = sb3("kcT", [128, 16, 128], BF16)
                kcTB = Buf()
                vcx = sb3("vcx", [128, 16, 2, 65], BF16)
                vcxB = Buf()
                skT_s = sb3("skT_s", [128, 64], BF16)
                svx_s = sb3("svx_s", [128, 2, 65], BF16)
                kvSB = Buf()
                qkT_s = sb3("qkT_s", [128, 8, 64], BF16)
                qkTSB = Buf()
                sqT_s = sb3("sqT_s", [128, 8, 64], BF16)
                sqTSB = Buf()
                scm_s = sb3("scm_s", [128, 512], BF16)
                scmSB = Buf()
                rq2 = sb3("rq2", [128, 8, 2, 64], BF16)
                rq2B = Buf()
                qg_sb = sb3("qg_sb", [128, 512], F32)
                qgSB = Buf()
                qx = sb3("qx", [128, 4, 8, 64], BF16)
                qxB = Buf()
                kx = sb3("kx", [128, 4, 16, 64], BF16)
                kxB = Buf()
                Sin = sb3("Sin", [128, 8, 4, 128], F32)
                SinB = Buf()
                Sb16 = sb3("Sb16", [128, 8, 4, 128], BF16)
                Sb16B = Buf()
                pTa = sb3("pTa", [128, 17, 256], BF16)
                pTaB = Buf()
                kvc = pTa[:, :, :].rearrange("p a b -> p (a b)")[:, 0:4096].bitcast(F32).rearrange("p (b c) -> p b c", b=16)
                kvcB = pTaB
                scq0 = sb3("scq0", [32, 1408], F32)
                scq = [scq0, scq0]
                scqB0 = Buf()
                scqB = [scqB0, scqB0]
                gcS = sb3("gcS", [128, NKC, 32], F32)
                gcSB = Buf()
                gtl, gtlB = gcS, gcSB
                gs6 = [sb3("gs6_%d" % i, [128, 16, 6], F32) for i in range(2)]
                gs6B = [Buf(), Buf()]

                slots = [(hS, hSB, NT)]
                dma(SP, hS[0:NT, :], xs, (), (hSB,))
                dma(SP, rtab[0][:, :], rope[32], (), (rtabB[0],))
                dma(SP, ks[:, 0:124, :], ck[:, 4:128, :], (), ())
                dma(SP, vs[:, 0:124, :], cv[:, 4:128, :], (), ())
                dma(SP, kvc[:, :, :], ck.rearrange("b s c -> s b c"), (), (kvcB,))
                for q4 in range(4):
                    use(PE, (kvcB, CONST), (PB[q4],))
                    for i in range(4):
                        ins = nc.tensor.transpose(bank(q4)[:, i * 128:(i + 1) * 128], kvc[:, q4 * 4 + i, :], identF[:])
                    fin(PE, ins, (kvcB,), (PB[q4],))
                    op(ACT, lambda: nc.scalar.copy(out=kcT[:, q4 * 4:q4 * 4 + 4, :].rearrange("p b s -> p (b s)"),
                                                   in_=bank(q4)[:, :]), (PB[q4],), (kcTB,))
                dma(SP, kvc[:, :, :], cv.rearrange("b s c -> s b c"), (), (kvcB,))
                op(DVE, lambda: nc.vector.memset(vcx[:], 1.0), (), (vcxB,))
                op(DVE, lambda: nc.vector.tensor_copy(out=vcx[:, :, :, 0:64],
                                                      in_=kvc[:, :, :].rearrange("p b (k d) -> p b k d", k=2)),
                   (kvcB,), (vcxB,))
                op(DVE, lambda: nc.vector.memset(svx_s[:], 1.0), (), (kvSB,))

                offs, T = phase_norm(slots, 176)
                ev = evac_in(slots, None)
                proj_tm(slots, offs, lambda c, k: xT[:, k, 0:NT], xTB, 16, tiles_wi, [0, 2, 1, 3, 6, 4, 7, 5, 8], ev)
                sB = slotB[0]
                use(PE, (sB, CONST), (PB[0],))
                for b in range(4):
                    nc.tensor.transpose(bank16(0)[:, b * 64:(b + 1) * 64], rqb[0][0:NT, b * 128:(b + 1) * 128],
                                        identB[0:NT, 0:NT])
                for b in range(4):
                    ins = nc.tensor.transpose(bank16(0)[:, (4 + b) * 64:(5 + b) * 64],
                                              kbb[0][0:NT, b * 128:(b + 1) * 128], identB[0:NT, 0:NT])
                fin(PE, ins, (sB,), (PB[0],))
                op(ACT, lambda: nc.scalar.copy(out=qkT_s[:].rearrange("p b t -> p (b t)"), in_=bank16(0)[:, 0:512]),
                   (PB[0],), (qkTSB,))
                use(PE, (qkTSB,), (PB[2], PB[3]))
                for h in range(8):
                    hp = (h % 2) * 64
                    ins = nc.tensor.matmul(bank(2 + h % 2)[0:NT, (h // 2) * 64:(h // 2 + 1) * 64],
                                           lhsT=qkT_s[hp:hp + 64, 4 + h // 2, :],
                                           rhs=qkT_s[hp:hp + 64, h // 2, :], start=True, stop=True)
                fin(PE, ins, (qkTSB,), (PB[2], PB[3]))
                for i in range(2):
                    op(DVE, lambda: nc.vector.tensor_tensor(out=scm_s[0:NT, i * 256:(i + 1) * 256],
                                                            in0=bank(2 + i)[0:NT, 0:256],
                                                            in1=t_dmask[0:NT, i * 256:(i + 1) * 256],
                                                            op=ALU.mult), (PB[2 + i], CONST), (scmSB,))
                op(DVE, lambda: nc.vector.tensor_copy(
                    out=rq2[0:NT, :, :, :],
                    in_=rqb[0][0:NT, :].rearrange("p (h d) -> p h d", h=8).unsqueeze(2).to_broadcast([NT, 8, 2, 64])),
                   (sB,), (rq2B,))
                use(PE, (rq2B, CONST), (PB[3],))
                for h in range(8):
                    ins = nc.tensor.transpose(bank16(3)[:, h * 64:(h + 1) * 64],
                                              rq2[0:NT, h, :, :].rearrange("p a d -> p (a d)"), identB[0:NT, 0:NT])
                fin(PE, ins, (rq2B,), (PB[3],))
                op(DVE, lambda: nc.vector.tensor_tensor(out=qg_sb[:, :], in0=bank16(3)[:, 0:512], in1=t_gq[:, :],
                                                        op=ALU.mult), (PB[3], CONST), (qgSB,))
                for hh in range(2):
                    for p2 in range(2):
                        for b2 in range(8):
                            dma(SP, Sin[p2 * 64:(p2 + 1) * 64, b2, :, :],
                                sret[2 * b2 + p2, hh * 4:(hh + 1) * 4, :, :].rearrange("h d v -> d h v"),
                                (), (SinB,))
                    for b2 in range(8):
                        op(ACT, lambda: nc.scalar.copy(out=Sb16[:, b2, :, :], in_=Sin[:, b2, :, :]), (SinB,), (Sb16B,))
                    for hl in range(4):
                        h = hh * 4 + hl
                        op(DVE, lambda: nc.vector.tensor_tensor(
                            out=qx[:, hl, :, :], in0=qg_sb[:, h * 64:(h + 1) * 64].unsqueeze(1).to_broadcast([128, 8, 64]),
                            in1=t_sel2[:, :].rearrange("p (b t) -> p b t", b=8), op=ALU.mult),
                           (qgSB, CONST), (qxB,))
                        op(DVE, lambda: nc.vector.tensor_tensor(
                            out=kx[0:NT, hl, :, :],
                            in0=kdec[0][0:NT, h * 64:(h + 1) * 64].unsqueeze(1).to_broadcast([NT, 16, 64]),
                            in1=t_selk[0:NT, :].unsqueeze(2).to_broadcast([NT, 16, 64]), op=ALU.mult),
                           (sB, CONST), (kxB,))
                    use(PE, (scmSB, sB, qxB, Sb16B), (PB[4 + hh],))
                    for hl in range(4):
                        h = hh * 4 + hl
                        o_ap = bank(4 + hh)[0:NT, hl * 128:(hl + 1) * 128]
                        hb = (h % 2) * 4 + h // 2
                        nc.tensor.matmul(o_ap, lhsT=scm_s[0:NT, hb * 64:(hb + 1) * 64],
                                         rhs=rvb[0][0:NT, h * 128:(h + 1) * 128], start=True, stop=False)
                        for b2 in range(8):
                            ins = nc.tensor.matmul(o_ap, lhsT=qx[:, hl, b2, :], rhs=Sb16[:, b2, hl, :],
                                                   start=False, stop=(b2 == 7))
                    fin(PE, ins, (scmSB, sB, qxB, Sb16B), (PB[4 + hh],))
                    for b2 in range(8):
                        bk = b2 % 2
                        use(PE, (kxB, sB), (PB[bk],))
                        for hl in range(4):
                            h = hh * 4 + hl
                            ins = nc.tensor.matmul(bank(bk)[:, hl * 128:(hl + 1) * 128],
                                                   lhsT=kx[0:NT, hl, 2 * b2:2 * b2 + 2, :].rearrange("p a d -> p (a d)"),
                                                   rhs=rvb[0][0:NT, h * 128:(h + 1) * 128], start=True, stop=True)
                        fin(PE, ins, (kxB, sB), (PB[bk],))
                        sv_ = Sin[:, b2, :, :]
                        op(DVE, lambda: nc.vector.tensor_tensor(
                            out=sv_, in0=sv_, in1=t_g4[:, hh * 4:(hh + 1) * 4].unsqueeze(2).to_broadcast([128, 4, 128]),
                            op=ALU.mult), (SinB, CONST), (SinB,))
                        op(DVE, lambda: nc.vector.tensor_tensor(
                            out=sv_, in0=sv_, in1=bank(bk)[:, :].rearrange("p (h v) -> p h v", h=4), op=ALU.add),
                           (SinB, PB[bk]), (SinB,))
                    for p2 in range(2):
                        for b2 in range(8):
                            dma(SP, rss[2 * b2 + p2, hh * 4:(hh + 1) * 4, :, :].rearrange("h d v -> d h v"),
                                Sin[p2 * 64:(p2 + 1) * 64, b2, :, :], (SinB,), ())
                gn_gate(0, NT, rgs[0], cat_s, catSB)
                use(PE, (sB, CONST), (PB[1],))
                for g in range(8):
                    ins = nc.tensor.transpose(bank16(1)[:, g * 64:(g + 1) * 64], sqb[0][0:NT, g * 128:(g + 1) * 128],
                                              identB[0:NT, 0:NT])
                fin(PE, ins, (sB,), (PB[1],))
                op(ACT, lambda: nc.scalar.copy(out=sqT_s[:].rearrange("p b t -> p (b t)"), in_=bank16(1)[:, 0:512]),
                   (PB[1],), (sqTSB,))
                op(PE, lambda: nc.tensor.transpose(bank16(7)[:, 0:64], skb[0][0:NT, :], identB[0:NT, 0:NT]),
                   (sB, CONST), (PB[7],))
                op(ACT, lambda: nc.scalar.copy(out=skT_s[:, :], in_=bank16(7)[:, 0:64]), (PB[7],), (kvSB,))
                op(ACT, lambda: nc.scalar.copy(out=svx_s[0:NT, :, 0:64],
                                               in_=svf[0][0:NT, :].rearrange("p (k d) -> p k d", k=2)),
                   (sB,), (kvSB,))
                bi = 0
                for kvh in range(2):
                    kp0 = kvh * 64
                    for gh in range(2):
                        for kb in range(17):
                            nk = 128 if kb < 16 else NT
                            bb = bi % 4
                            bi += 1
                            lhs = kcT[kp0:kp0 + 64, kb, :] if kb < 16 else skT_s[kp0:kp0 + 64, :]
                            op(PE, lambda: nc.tensor.matmul(bank(bb)[0:nk, 0:256], lhsT=lhs,
                                                            rhs=sqT_s[kp0:kp0 + 64, gh * 4:gh * 4 + 4, :],
                                                            start=True, stop=True),
                               (kcTB, kvSB, sqTSB), (PB[bb],))
                            op(ACT, lambda: nc.scalar.activation(out=pTa[0:nk, kb, :], in_=bank(bb)[0:nk, 0:256],
                                                                 func=AF.Exp, scale=0.125), (PB[bb],), (pTaB,))
                            mk = (t_mc[:, kb * 64:(kb + 1) * 64] if kb < 16 else t_mn[0:NT, :])
                            op(POOL, lambda: nc.gpsimd.tensor_tensor(
                                out=pTa[0:nk, kb, :].rearrange("p (g t) -> p g t", g=4),
                                in0=pTa[0:nk, kb, :].rearrange("p (g t) -> p g t", g=4),
                                in1=mk.unsqueeze(1).to_broadcast([nk, 4, 64]), op=ALU.mult),
                               (pTaB, CONST), (pTaB,))
                        wr = (PB[4], PB[5], PB[6]) if (kvh == 0 and gh == 0) else ()
                        use(PE, (pTaB, vcxB, kvSB), wr)
                        for gl in range(4):
                            hq = kvh * 8 + gh * 4 + gl
                            o_ap = bank(4 + hq // 7)[0:NT, (hq % 7) * 65:(hq % 7) * 65 + 65]
                            for kb in range(17):
                                nk = 128 if kb < 16 else NT
                                rhs = vcx[:, kb, kvh, :] if kb < 16 else svx_s[0:NT, kvh, :]
                                ins = nc.tensor.matmul(o_ap, lhsT=pTa[0:nk, kb, gl * 64:(gl + 1) * 64], rhs=rhs,
                                                       start=(kb == 0), stop=(kb == 16))
                        fin(PE, ins, (pTaB, vcxB, kvSB), (PB[4], PB[5], PB[6]))
                swa_norm(NT, cat_s, catSB)
                cat_to_xT(NT, cat_s, catSB, 0)
                for b in range(16):
                    dma(SP, ks[b, 124:128, :], skf[0][4 * b:4 * b + 4, :], (sB,), ())
                    dma(SP, vs[b, 124:128, :], svf[0][4 * b:4 * b + 4, :], (sB,), ())

                def ev_os(nb, c, pa, pB):
                    op(DVE, lambda: nc.vector.tensor_tensor(out=hS[0:NT, nb * 512:(nb + 1) * 512], in0=pa[0:NT, :],
                                                            in1=hS[0:NT, nb * 512:(nb + 1) * 512], op=ALU.add),
                       (pB, hSB), (hSB,))
                proj_tm(slots, offs, lambda c, k: xT[:, k, 0:NT], xTB, 16, tiles_wo, list(range(4)), ev_os, set0=1)
                offs, T = phase_norm(slots, 192)
                for q4 in range(4):
                    sc_, scB_ = scq[q4 % 2], scqB[q4 % 2]
                    dma(SP, sc_[:, :], sconv[:, q4 * 1408:(q4 + 1) * 1408], (), (scB_,))
                    bk = q4 % 2
                    use(PE, (scB_, CONST), (PB[bk],))
                    for i in range(11):
                        ins = nc.tensor.transpose(bank(bk)[:, i * 32:(i + 1) * 32], sc_[:, i * 128:(i + 1) * 128],
                                                  identF[0:32, 0:32])
                    fin(PE, ins, (scB_,), (PB[bk],))
                    op(ACT, lambda: nc.scalar.copy(out=gcS[:, q4 * 11:(q4 + 1) * 11, :].rearrange("p k t -> p (k t)"),
                                                   in_=bank(bk)[:, 0:352]), (PB[bk],), (gcSB,))
                for j in range(22):
                    tl = []
                    for gu in range(2):
                        for kt in range(2):
                            tl.append(wload(wu[gu, j, :, kt * 8:(kt + 1) * 8, :], WU[gu][j], 8, 256))
                    s_ = j % 2
                    banks = [4 * s_, 4 * s_ + 1, 4 * s_ + 2, 4 * s_ + 3]
                    for gu in range(2):
                        for kt in range(2):
                            wt, wtB = tl[gu * 2 + kt]
                            wr = [PB[banks[gu * 2]], PB[banks[gu * 2 + 1]]] if kt == 0 else []
                            use(PE, (wtB, xTB), wr)
                            for f in range(2):
                                for kk in range(8):
                                    k = kt * 8 + kk
                                    ins = nc.tensor.matmul(bank(banks[gu * 2 + f])[:, 0:NT],
                                                           lhsT=wt[:, kk, f * 128:(f + 1) * 128], rhs=xT[:, k, 0:NT],
                                                           start=(k == 0), stop=(k == 15))
                            fin(PE, ins, (wtB, xTB), [PB[banks[gu * 2]], PB[banks[gu * 2 + 1]]])
                    for f in range(2):
                        kc = 2 * j + f
                        bg, bu = banks[f], banks[2 + f]
                        gi = kc % 2
                        g6, g6B = gs6[gi], gs6B[gi]
                        a_, aB = cva[gi], cvaB[gi]
                        gps = bank(bg)[:, 0:NT].rearrange("p (b t) -> p b t", t=4)
                        a3 = a_[:, 0:NT].rearrange("p (b t) -> p b t", t=4)
                        op(ACT, lambda: nc.scalar.copy(out=g6[:, :, 2:6], in_=gps), (PB[bg],), (g6B,))
                        op(ACT, lambda: nc.scalar.activation(out=a_[:, 0:NT], in_=bank(bg)[:, 0:NT], func=AF.Identity,
                                                             scale=cwcol(2, kc), bias=cbcol(kc)), (PB[bg], CONST), (aB,))
                        op(DVE, lambda: nc.vector.tensor_copy(out=g6[:, :, 0:2],
                                                              in_=gcS[:, kc, :].rearrange("p (b t) -> p b t", t=2)),
                           (gcSB,), (g6B,))
                        op(DVE, lambda: nc.vector.tensor_copy(out=gtl[:, kc, :].rearrange("p (b t) -> p b t", t=2),
                                                              in_=g6[:, :, 4:6]), (g6B,), (gtlB,))
                        op(DVE, lambda: nc.vector.scalar_tensor_tensor(out=a3, in0=g6[:, :, 1:5], scalar=cwcol(1, kc),
                                                                       in1=a3, op0=ALU.mult, op1=ALU.add),
                           (g6B, aB, CONST), (aB,))
                        op(DVE, lambda: nc.vector.scalar_tensor_tensor(out=a3, in0=g6[:, :, 0:4], scalar=cwcol(0, kc),
                                                                       in1=a3, op0=ALU.mult, op1=ALU.add),
                           (g6B, aB, CONST), (aB,))
                        op(ACT, lambda: nc.scalar.activation(out=a_[:, 0:NT], in_=a_[:, 0:NT], func=AF.Silu), (aB,), (aB,))
                        op(DVE, lambda: nc.vector.tensor_tensor(out=act_s[:, kc, :], in0=a_[:, 0:NT], in1=bank(bu)[:, 0:NT],
                                                                op=ALU.mult), (aB, PB[bu]), (actSB,))
                for q4 in range(4):
                    sc_, scB_ = scq[q4 % 2], scqB[q4 % 2]
                    for t3 in range(3):
                        k0 = q4 * 11 + t3 * 4
                        nk4 = 4 if t3 < 2 else 3
                        bk = (q4 * 3 + t3) % 2
                        use(PE, (gtlB, CONST), (PB[bk],))
                        for i in range(nk4):
                            ins = nc.tensor.transpose(bank(bk)[0:32, i * 128:(i + 1) * 128], gtl[:, k0 + i, :], identF[:])
                        fin(PE, ins, (gtlB,), (PB[bk],))
                        op(ACT, lambda: nc.scalar.copy(out=sc_[:, t3 * 512:t3 * 512 + nk4 * 128],
                                                       in_=bank(bk)[0:32, 0:nk4 * 128]), (PB[bk],), (scB_,))
                    dma(SP, cs[:, q4 * 1408:(q4 + 1) * 1408], sc_[:, :], (scB_,), ())

                def ev_ds(nb, c, pa, pB):
                    op(DVE, lambda: nc.vector.tensor_tensor(out=hS[0:NT, nb * 512:(nb + 1) * 512], in0=pa[0:NT, :],
                                                            in1=hS[0:NT, nb * 512:(nb + 1) * 512], op=ALU.add),
                       (pB, hSB), (hSB,))
                    if nb == 3:
                        dma(SP, ys[:, :], hS[0:NT, :], (hSB,), ())
                proj_tm(slots, offs, lambda c, k: act_s[:, k, :], actSB, NKC, tiles_wd, list(range(4)), ev_ds)
                barrier()
        barrier()
    return nc


def _tables():
    f = np.float32
    lg = np.log1p(-np.exp2(-5.0 - np.arange(8, dtype=f))).astype(f)
    i = np.arange(128, dtype=f)
    diff = i[None, :] - i[:, None]
    dm = np.where(diff[:, None, :] >= 0, np.exp(np.maximum(diff, 0)[:, None, :] * lg[None, :, None]), 0.0)
    c_dmask = np.ascontiguousarray(dm[:, [0, 2, 4, 6, 1, 3, 5, 7], :]).reshape(128, 1024).astype(f)
    p = np.arange(128)
    gq = np.zeros((128, 4, 128), f)
    for b in range(4):
        h = 2 * b + p // 64
        gq[:, b, :] = np.exp((i[None, :] + 1.0) * lg[h][:, None])
    c_gq = gq.reshape(128, 512)
    c_kd = np.exp((127.0 - i)[:, None] * lg[None, :]).astype(f)
    gS = np.zeros((128, 4), f)
    for b in range(4):
        gS[:, b] = np.exp(128.0 * lg[2 * b + p // 64])
    mcur = (i[:, None] <= i[None, :]).astype(f)
    mprev = (i[:, None] >= i[None, :]).astype(f)
    return dict(c_dmask=c_dmask, c_gq=c_gq, c_kd=c_kd, c_gS=gS, c_mcur=mcur, c_mprev=mprev), lg


def _rope_tab(pos):
    f = np.float32
    half = 32
    inv = (f(10000.0) ** (-(np.arange(half, dtype=f)) / f(half))).astype(f)
    ang = (pos.astype(f)[:, None] * inv[None, :]).astype(f)
    c = np.cos(ang).astype(f)
    s = np.sin(ang).astype(f)
    return np.concatenate([c, s, c * f(0.125), s * f(0.125)], axis=1).astype(f)


def _sample_tables(lg):
    f = np.float32
    tok = np.arange(64)
    b = tok // 4
    t = (tok % 4).astype(f)
    same = (b[:, None] == b[None, :])
    dm = np.zeros((64, 8, 64), f)
    for h in range(8):
        d = t[None, :] - t[:, None]
        dm[:, h, :] = np.where(same & (d >= 0), np.exp(np.maximum(d, 0) * lg[h]), 0.0)
    gq = np.zeros((128, 8, 64), f)
    for h in range(8):
        gq[:, h, :] = np.exp((t + 1.0) * lg[h])[None, :]
    kd = np.exp((3.0 - t)[:, None] * lg[None, :]).astype(f)
    g4 = np.tile(np.exp(4.0 * lg)[None, :], (128, 1)).astype(f)
    sel2 = np.zeros((128, 8, 64), f)
    for p2 in range(2):
        for b2 in range(8):
            sel2[p2 * 64:(p2 + 1) * 64, b2, :] = (b == 2 * b2 + p2).astype(f)[None, :]
    selk = (b[:, None] == np.arange(16)[None, :]).astype(f)
    s_ = np.arange(128)
    mc = np.zeros((128, 16, 64), f)
    for bb in range(16):
        mc[:, bb, :] = ((b[None, :] == bb) & (s_[:, None] >= (tok % 4)[None, :])).astype(f)
    mn = (same & ((tok % 4)[:, None] <= (tok % 4)[None, :])).astype(f)
    dm = np.ascontiguousarray(dm[:, [0, 2, 4, 6, 1, 3, 5, 7], :])
    return dict(s_dmask=dm.reshape(64, 512), s_gq=gq.reshape(128, 512), s_kd=kd, s_g4=g4,
                s_sel2=sel2.reshape(128, 512), s_selk=selk, s_mc=mc.reshape(128, 1024), s_mn=mn)


_NC_CACHE = {}


def kernel(x_prompt, x_sample, state_ret, cache_swa_k, cache_swa_v, state_conv,
           attn_norm_w, w_in, swa_q_norm_w, swa_k_norm_w, swa_sinks, ret_gn_w, w_o,
           ffn_norm_w, w_up, conv_w, conv_b, w_down):
    f = np.float32
    A = lambda a: np.ascontiguousarray(np.asarray(a), dtype=f)
    x_prompt, x_sample = A(x_prompt), A(x_sample)
    state_ret, cache_swa_k, cache_swa_v, state_conv = A(state_ret), A(cache_swa_k), A(cache_swa_v), A(state_conv)
    w_in2, w_o2, w_up2, w_down2 = A(w_in)[0], A(w_o)[0], A(w_up)[0], A(w_down)[0]
    prm = np.concatenate([A(conv_w)[0].reshape(132, 128), A(conv_b)[0].reshape(44, 128),
                          A(attn_norm_w)[0].reshape(16, 128), A(ffn_norm_w)[0].reshape(16, 128)], axis=0)
    qkn = np.concatenate([A(swa_q_norm_w)[0], A(swa_k_norm_w)[0]])[None, :]
    sinks = A(swa_sinks)[0][None, :]
    gnw = A(ret_gn_w)[0][None, :]
    ct, lg = _tables()
    stb = _sample_tables(lg)
    if "nc" not in _NC_CACHE:
        _NC_CACHE["nc"] = build()
    nc = _NC_CACHE["nc"]
    in_maps = []
    for c in range(NCORES):
        b, half = c // 2, c % 2
        if half == 0:
            xcv = np.concatenate([np.zeros((2048, D), f), x_prompt[b, :2048]], axis=0)
        else:
            xcv = x_prompt[b]
        pos0 = half * 2048 - 2048
        rp = np.zeros((33, 128, 128), f)
        for ci in range(32):
            pos = pos0 + ci * 128 + np.arange(128)
            rp[ci] = _rope_tab(np.maximum(pos, 0))
        rp[32, :64] = _rope_tab(16384 + (np.arange(64) % 4))
        m = dict(
            xc=np.ascontiguousarray(xcv), xs=np.ascontiguousarray(x_sample[c * 16:(c + 1) * 16].reshape(64, D)),
            sret=np.ascontiguousarray(state_ret[0, c * 16:(c + 1) * 16]),
            ck=np.ascontiguousarray(cache_swa_k[0, c * 16:(c + 1) * 16].reshape(16, 128, 128)),
            cv=np.ascontiguousarray(cache_swa_v[0, c * 16:(c + 1) * 16].reshape(16, 128, 128)),
            sconv=np.ascontiguousarray(state_conv[0, c * 16:(c + 1) * 16].reshape(32, DFF)),
            w_in=w_in2, w_o=w_o2, w_up=w_up2, w_down=w_down2, prm=prm, qkn=qkn, sinks=sinks, gnw=gnw, rope=rp,
            c_mprev1=(ct["c_mprev"] if half == 1 else np.zeros((128, 128), f)),
        )
        m.update(ct)
        m.update(stb)
        in_maps.append(m)
    res = run_bass_kernel_spmd(nc, in_maps, core_ids=list(range(NCORES)))
    R = res.results
    y_prompt = np.stack([np.concatenate([R[2 * b]["yp"], R[2 * b + 1]["yp"]], axis=0) for b in range(4)])
    y_sample = np.concatenate([R[c]["ys"].reshape(16, 4, D) for c in range(NCORES)], axis=0)
    ret_p = np.stack([R[2 * b + 1]["rsp"] for b in range(4)])[None]
    ret_s = np.concatenate([R[c]["rss"] for c in range(NCORES)], axis=0)[None]
    k_p = np.stack([R[2 * b + 1]["kp"].reshape(128, 2, 64) for b in range(4)])[None]
    v_p = np.stack([R[2 * b + 1]["vp"].reshape(128, 2, 64) for b in range(4)])[None]
    k_s = np.concatenate([R[c]["ks"].reshape(16, 128, 2, 64) for c in range(NCORES)], axis=0)[None]
    v_s = np.concatenate([R[c]["vs"].reshape(16, 128, 2, 64) for c in range(NCORES)], axis=0)[None]
    c_p = np.stack([R[2 * b + 1]["cp"] for b in range(4)])[None]
    c_s = np.concatenate([R[c]["cs"].reshape(16, 2, DFF) for c in range(NCORES)], axis=0)[None]
    out = (y_prompt, y_sample, ret_p, ret_s, k_p, k_s, v_p, v_s, c_p, c_s)
    return tuple(np.ascontiguousarray(o, dtype=f) for o in out)
```
